# Optimizing a Trainium2 kernel written in Bass

```python
import math
import jax, jax.numpy as jnp
from jax import lax
import numpy as np

D_MODEL = 1024
BATCH = 16
SEQ = 256
DEPTH = 4
DEC_BATCH = 4
DEC_SEQ = 1024
PAST_LEN = 256

GRID_W = 64
N_MIXERS = 2
N_DIFF_LAYERS = (DEPTH + 1) // 2
N_NA_LAYERS = DEPTH // 2
DIFF_DK = 64
DIFF_DV = 2 * DIFF_DK
DIFF_HEADS = D_MODEL // (2 * DIFF_DK)
NA_DH = 64
NA_HEADS = D_MODEL // NA_DH
NA_KH_MAX = 8
NA_KW = 16
NA_QB_W = 16
NA_KB_W = NA_QB_W + NA_KW
D_FF = 4 * D_MODEL
ROPE_BASE = 10000.0
EPS = 1e-6
NEG_INF = -1e30

kernel_name = "hybrid_diff_natten_prefix_dit_step"


def rms_norm(x, g):
    xf = x.astype(jnp.float32)
    y = xf * lax.rsqrt(jnp.mean(xf * xf, axis=-1, keepdims=True) + EPS)
    return (y * g.astype(jnp.float32)).astype(x.dtype)


def modulation(cond, w_ada_l, b_ada_l):
    m = (jax.nn.silu(cond) @ w_ada_l + b_ada_l)[:, None, :]
    return jnp.split(m, 6, axis=-1)


def mlp_residual(x, shift, scale, gate, g, w1, w2):
    h = rms_norm(x, g) * (1 + scale) + shift
    return x + gate * (jnp.square(jax.nn.relu(h @ w1)) @ w2)


def rope_half(x, ang):
    n = x.shape[-1] // 2
    cos = jnp.cos(ang).astype(x.dtype)
    sin = jnp.sin(ang).astype(x.dtype)
    x1, x2 = x[..., :n], x[..., n:]
    return jnp.concatenate([x1 * cos - x2 * sin, x2 * cos + x1 * sin], axis=-1)


def rope_2d(x, ang_row, ang_col):
    half = x.shape[-1] // 2
    return jnp.concatenate([rope_half(x[..., :half], ang_row), rope_half(x[..., half:], ang_col)], axis=-1)


def diff_qkv(h, w_qkv, q_g, k_g):
    B, T, _ = h.shape
    q, k, v = jnp.split(h @ w_qkv, 3, axis=-1)
    q = rms_norm(q.reshape(B, T, DIFF_HEADS, 2, DIFF_DK), q_g)
    k = rms_norm(k.reshape(B, T, DIFF_HEADS, 2, DIFF_DK), k_g)
    v = v.reshape(B, T, DIFF_HEADS, DIFF_DV)
    return q, k, v


def diff_lambda(lq1, lk1, lq2, lk2, lambda_init):
    e1 = jnp.exp(jnp.sum(lq1.astype(jnp.float32) * lk1.astype(jnp.float32)))
    e2 = jnp.exp(jnp.sum(lq2.astype(jnp.float32) * lk2.astype(jnp.float32)))
    return e1 - e2 + lambda_init


def diff_attend(q, k, v, lam, lambda_init, g_sub):
    B, Tq = q.shape[0], q.shape[1]
    s = jnp.einsum('bqhmd,bkhmd->bhmqk', q, k).astype(jnp.float32) * (DIFF_DK ** -0.5)
    p = jax.nn.softmax(s, axis=-1)
    a = p[:, :, 0] - lam * p[:, :, 1]
    o = jnp.einsum('bhqk,bkhe->bqhe', a.astype(v.dtype), v)
    o = rms_norm(o, g_sub) * (1.0 - lambda_init)
    return o.reshape(B, Tq, DIFF_HEADS * DIFF_DV)


def na_qkv(h, w_qkv, q_g, k_g):
    B, T, _ = h.shape
    q, k, v = jnp.split(h @ w_qkv, 3, axis=-1)
    q = rms_norm(q.reshape(B, T, NA_HEADS, NA_DH), q_g)
    k = rms_norm(k.reshape(B, T, NA_HEADS, NA_DH), k_g)
    v = v.reshape(B, T, NA_HEADS, NA_DH)
    return q, k, v


def dense_attend(q, k, v):
    B, Tq, H, Dh = q.shape
    s = jnp.einsum('bqhd,bkhd->bhqk', q, k).astype(jnp.float32) * (Dh ** -0.5)
    p = jax.nn.softmax(s, axis=-1)
    o = jnp.einsum('bhqk,bkhd->bqhd', p.astype(v.dtype), v)
    return o.reshape(B, Tq, H * Dh)


def na_latent_attend(q, k, v, kc, vc, rel_bias):
    B, T, H, Dh = q.shape
    rows = T // GRID_W
    kh = min(NA_KH_MAX, rows)
    ncb = GRID_W // NA_QB_W
    r = jnp.arange(rows)
    rs = jnp.clip(r - kh // 2, 0, rows - kh)
    row_idx = rs[:, None] + jnp.arange(kh)
    j = jnp.arange(ncb)
    kb = jnp.clip(j * NA_QB_W - NA_KW // 2, 0, GRID_W - NA_KB_W)
    col_idx = kb[:, None] + jnp.arange(NA_KB_W)
    qcol = j[:, None] * NA_QB_W + jnp.arange(NA_QB_W)
    cs = jnp.clip(qcol - NA_KW // 2, 0, GRID_W - NA_KW)
    kcol = col_idx[:, None, :]
    valid = (kcol >= cs[..., None]) & (kcol < cs[..., None] + NA_KW)
    nw = kh * NA_KB_W
    mask = jnp.broadcast_to(valid[:, :, None, :], (ncb, NA_QB_W, kh, NA_KB_W)).reshape(ncb, NA_QB_W, nw)
    dr = row_idx - r[:, None] + (NA_KH_MAX - 1)
    dc = jnp.clip(kcol - qcol[..., None], -(NA_KW - 1), NA_KW - 1) + (NA_KW - 1)
    bias = rel_bias[:, dr[:, None, None, :, None], dc[None, :, :, None, :]]
    bias = bias.reshape(H, rows, ncb, NA_QB_W, nw).astype(jnp.float32)
    qg = q.reshape(B, rows, ncb, NA_QB_W, H, Dh)
    ridx = row_idx[:, None, :, None]
    cidx = col_idx[None, :, None, :]
    kg = k.reshape(B, rows, GRID_W, H, Dh)[:, ridx, cidx].reshape(B, rows, ncb, nw, H, Dh)
    vg = v.reshape(B, rows, GRID_W, H, Dh)[:, ridx, cidx].reshape(B, rows, ncb, nw, H, Dh)
    scale = Dh ** -0.5
    s_win = jnp.einsum('brjqhd,brjkhd->bhrjqk', qg, kg).astype(jnp.float32) * scale + bias[None]
    s_win = jnp.where(mask[None, None, None], s_win, NEG_INF)
    s_ctx = jnp.einsum('brjqhd,bkhd->bhrjqk', qg, kc).astype(jnp.float32) * scale
    p = jax.nn.softmax(jnp.concatenate([s_win, s_ctx], axis=-1), axis=-1)
    o = (jnp.einsum('bhrjqk,brjkhd->brjqhd', p[..., :nw].astype(v.dtype), vg)
         + jnp.einsum('bhrjqk,bkhd->brjqhd', p[..., nw:].astype(v.dtype), vc))
    return o.reshape(B, T, H * Dh)


def setup_inputs(seed: int = 0) -> dict:
    key = jax.random.key(seed)
    ks = jax.random.split(key, 40)

    def nrm(k, shape, scale=1.0):
        return jax.random.normal(k, shape, jnp.float32) * scale

    D = D_MODEL
    return {
        "x_prompt": nrm(ks[0], (BATCH, SEQ, D)),
        "x_sample": nrm(ks[1], (DEC_BATCH, DEC_SEQ, D)),
        "cache_diff_k": nrm(ks[2], (DEC_BATCH, N_DIFF_LAYERS, PAST_LEN, DIFF_HEADS, 2 * DIFF_DK)),
        "cache_diff_v": nrm(ks[3], (DEC_BATCH, N_DIFF_LAYERS, PAST_LEN, DIFF_HEADS, DIFF_DV)),
        "cache_na_k": nrm(ks[4], (DEC_BATCH, N_NA_LAYERS, PAST_LEN, NA_HEADS, NA_DH)),
        "cache_na_v": nrm(ks[5], (DEC_BATCH, N_NA_LAYERS, PAST_LEN, NA_HEADS, NA_DH)),
        "c": nrm(ks[6], (DEC_BATCH, D)),
        "c_ctx": nrm(ks[7], (D,)),
        "w_ada": nrm(ks[8], (DEPTH, D, 6 * D), 0.5 * D ** -0.5),
        "b_ada": nrm(ks[9], (DEPTH, 6 * D), 0.01),
        "norm_mix_g": 1.0 + nrm(ks[10], (DEPTH, D), 0.02),
        "norm_mlp_g": 1.0 + nrm(ks[11], (DEPTH, D), 0.02),
        "w_fc1": nrm(ks[12], (DEPTH, D, D_FF), D ** -0.5),
        "w_fc2": nrm(ks[13], (DEPTH, D_FF, D), D_FF ** -0.5),
        "w_qkv_diff": nrm(ks[14], (N_DIFF_LAYERS, D, 3 * D), D ** -0.5),
        "w_o_diff": nrm(ks[15], (N_DIFF_LAYERS, DIFF_HEADS * DIFF_DV, D), (DIFF_HEADS * DIFF_DV) ** -0.5),
        "q_norm_diff_g": 1.0 + nrm(ks[16], (N_DIFF_LAYERS, DIFF_DK), 0.02),
        "k_norm_diff_g": 1.0 + nrm(ks[17], (N_DIFF_LAYERS, DIFF_DK), 0.02),
        "lambda_q1": nrm(ks[18], (N_DIFF_LAYERS, DIFF_DK), 0.1),
        "lambda_k1": nrm(ks[19], (N_DIFF_LAYERS, DIFF_DK), 0.1),
        "lambda_q2": nrm(ks[20], (N_DIFF_LAYERS, DIFF_DK), 0.1),
        "lambda_k2": nrm(ks[21], (N_DIFF_LAYERS, DIFF_DK), 0.1),
        "subln_g": 1.0 + nrm(ks[22], (N_DIFF_LAYERS, DIFF_DV), 0.02),
        "w_qkv_na": nrm(ks[23], (N_NA_LAYERS, D, 3 * D), D ** -0.5),
        "w_o_na": nrm(ks[24], (N_NA_LAYERS, NA_HEADS * NA_DH, D), (NA_HEADS * NA_DH) ** -0.5),
        "q_norm_na_g": 1.0 + nrm(ks[25], (N_NA_LAYERS, NA_DH), 0.02),
        "k_norm_na_g": 1.0 + nrm(ks[26], (N_NA_LAYERS, NA_DH), 0.02),
        "rel_bias_na": nrm(ks[27], (N_NA_LAYERS, NA_HEADS, 2 * NA_KH_MAX - 1, 2 * NA_KW - 1), 0.1),
    }


def reference(x_prompt, x_sample, cache_diff_k, cache_diff_v, cache_na_k, cache_na_v, c, c_ctx,
              w_ada, b_ada, norm_mix_g, norm_mlp_g, w_fc1, w_fc2,
              w_qkv_diff, w_o_diff, q_norm_diff_g, k_norm_diff_g,
              lambda_q1, lambda_k1, lambda_q2, lambda_k2, subln_g,
              w_qkv_na, w_o_na, q_norm_na_g, k_norm_na_g, rel_bias_na):
    Bd, T = x_sample.shape[0], x_sample.shape[1]
    L = cache_diff_k.shape[2]
    n_freq = DIFF_DK // 4
    freqs = ROPE_BASE ** (-jnp.arange(n_freq, dtype=jnp.float32) / n_freq)
    t = jnp.arange(T)
    ang_row = ((t // GRID_W).astype(jnp.float32)[:, None] * freqs).reshape(T, 1, 1, n_freq)
    ang_col = ((t % GRID_W).astype(jnp.float32)[:, None] * freqs).reshape(T, 1, 1, n_freq)
    cond_ctx = c_ctx[None, :]

    yp, ys = x_prompt, x_sample
    new_dk, new_dv, new_nk, new_nv = [], [], [], []
    for l in range(DEPTH):
        i = l // N_MIXERS
        mp = modulation(cond_ctx, w_ada[l], b_ada[l])
        ms = modulation(c, w_ada[l], b_ada[l])
        hp = rms_norm(yp, norm_mix_g[l]) * (1 + mp[1]) + mp[0]
        hs = rms_norm(ys, norm_mix_g[l]) * (1 + ms[1]) + ms[0]
        if l % N_MIXERS == 0:
            lambda_init = 0.8 - 0.6 * math.exp(-0.3 * l)
            lam = diff_lambda(lambda_q1[i], lambda_k1[i], lambda_q2[i], lambda_k2[i], lambda_init)
            qp, kp, vp = diff_qkv(hp, w_qkv_diff[i], q_norm_diff_g[i], k_norm_diff_g[i])
            op = diff_attend(qp, kp, vp, lam, lambda_init, subln_g[i])
            new_dk.append(kp.reshape(kp.shape[0], kp.shape[1], DIFF_HEADS, 2 * DIFF_DK))
            new_dv.append(vp)
            qs, kls, vls = diff_qkv(hs, w_qkv_diff[i], q_norm_diff_g[i], k_norm_diff_g[i])
            qs = rope_2d(qs, ang_row, ang_col)
            kls = rope_2d(kls, ang_row, ang_col)
            kc = cache_diff_k[:, i].reshape(Bd, L, DIFF_HEADS, 2, DIFF_DK)
            os_ = diff_attend(qs, jnp.concatenate([kls, kc], axis=1),
                              jnp.concatenate([vls, cache_diff_v[:, i]], axis=1),
                              lam, lambda_init, subln_g[i])
            yp = yp + mp[2] * (op @ w_o_diff[i])
            ys = ys + ms[2] * (os_ @ w_o_diff[i])
        else:
            qp, kp, vp = na_qkv(hp, w_qkv_na[i], q_norm_na_g[i], k_norm_na_g[i])
            op = dense_attend(qp, kp, vp)
            new_nk.append(kp)
            new_nv.append(vp)
            qs, kls, vls = na_qkv(hs, w_qkv_na[i], q_norm_na_g[i], k_norm_na_g[i])
            os_ = na_latent_attend(qs, kls, vls, cache_na_k[:, i], cache_na_v[:, i], rel_bias_na[i])
            yp = yp + mp[2] * (op @ w_o_na[i])
            ys = ys + ms[2] * (os_ @ w_o_na[i])
        yp = mlp_residual(yp, mp[3], mp[4], mp[5], norm_mlp_g[l], w_fc1[l], w_fc2[l])
        ys = mlp_residual(ys, ms[3], ms[4], ms[5], norm_mlp_g[l], w_fc1[l], w_fc2[l])

    return (yp, ys, jnp.stack(new_dk, axis=1), jnp.stack(new_dv, axis=1),
            jnp.stack(new_nk, axis=1), jnp.stack(new_nv, axis=1))
```

```python
import math
from contextlib import ExitStack

import numpy as np
import concourse.bass as bass
import concourse.mybir as mybir
from concourse.bass_utils import run_bass_kernel_spmd

F32 = mybir.dt.float32
BF16 = mybir.dt.bfloat16
AF = mybir.ActivationFunctionType
ALU = mybir.AluOpType

D = 1024
NT = 1024
DEPTH = 4
L_CTX = 256
NKEY = NT + L_CTX
EPS = 1e-6
BIGM = 32768.0
N_CORES = 8
EMBED_WAIT = True
POOL_ENG = "dve"
_STOP = None


class _Stop(Exception):
    pass


def _stage(name):
    if _STOP == name:
        raise _Stop()


class _Op:
    __slots__ = ("eng", "emit", "deps", "is_dma", "sig", "need_sig", "idx", "waits", "knows")

    def __init__(self, eng, emit, is_dma):
        self.eng = eng
        self.emit = emit
        self.deps = []
        self.is_dma = is_dma
        self.sig = None
        self.need_sig = is_dma
        self.idx = -1


class Sched:
    ENGS = ("pe", "act", "dve", "pool", "sp")
    EPOCH = 6000
    NDMA = {"sp": 12, "pool": 6}

    def __init__(self):
        self.ops = {e: [] for e in self.ENGS}
        self.last_w = {}
        self.readers = {}
        self.dma_hist = {"sp": [], "pool": []}
        self.order = []

    def add(self, eng, emit, reads=(), writes=(), dma=False):
        op = _Op(eng, emit, dma)
        deps = []
        for k in reads:
            w = self.last_w.get(k)
            if w is not None:
                deps.append(w)
            if k[0] == "ps":
                deps.extend(r for r in self.readers.get(k, ()) if r.eng != eng)
        for k in writes:
            w = self.last_w.get(k)
            if w is not None:
                deps.append(w)
            deps.extend(self.readers.get(k, ()))
        if dma:
            h = self.dma_hist[eng]
            n = self.NDMA[eng]
            if len(h) >= n:
                deps.append(h[-n])
            h.append(op)
        seen = set()
        for d in deps:
            if d is op or id(d) in seen:
                continue
            if d.eng == "pe" and eng == "pe" and not d.is_dma:
                continue
            seen.add(id(d))
            op.deps.append(d)
            d.need_sig = True
        for k in reads:
            self.readers.setdefault(k, []).append(op)
        for k in writes:
            self.last_w[k] = op
            self.readers[k] = []
        op.idx = len(self.ops[eng])
        self.ops[eng].append(op)
        self.order.append(op)
        return op

    def lower(self, nc, es):
        sem_cache = {}

        def get_sem(name):
            if name not in sem_cache:
                sem_cache[name] = es.enter_context(nc.semaphore(name))
            return sem_cache[name]

        for e in self.ENGS:
            cnt = 0
            dcnt = 0
            for op in self.ops[e]:
                if op.is_dma:
                    n = self.NDMA[e]
                    op.sig = ("d_%s_%d" % (e, dcnt % n), 16 * (dcnt // n + 1))
                    dcnt += 1
                elif op.need_sig:
                    op.sig = ("c_%s_%d" % (e, cnt // self.EPOCH), cnt % self.EPOCH + 1)
                    cnt += 1
        for e in self.ENGS:
            for op in self.ops[e]:
                if op.sig is not None:
                    get_sem(op.sig[0])
        block = es.enter_context(nc.Block())
        dec = {"pe": block.tensor, "act": block.scalar, "dve": block.vector, "pool": block.gpsimd,
               "sp": block.sync}

        known = {e: {} for e in self.ENGS}
        for op in self.order:
            K = known[op.eng]
            need = {}
            for d in op.deps:
                sname, v = d.sig
                if K.get(sname, 0) >= v:
                    continue
                if need.get(sname, 0) < v:
                    need[sname] = v
            for d in op.deps:
                for s2, v2 in d.knows.items():
                    if K.get(s2, 0) < v2:
                        K[s2] = v2
            for sname, v in need.items():
                if K.get(sname, 0) < v:
                    K[sname] = v
            op.waits = list(need.items())
            op.knows = dict(K)
            if op.sig is not None:
                op.knows[op.sig[0]] = max(op.knows.get(op.sig[0], 0), op.sig[1])

        def make(e):
            def body(eng):
                for op in self.ops[e]:
                    items = list(op.waits)
                    emb = None
                    if EMBED_WAIT and items:
                        emb = items.pop()
                    for sname, v in items:
                        eng.wait_ge(get_sem(sname), v)
                    ins = op.emit(eng)
                    if emb is not None:
                        ins._wait_ge(get_sem(emb[0]), emb[1])
                    if op.sig is not None:
                        ins.then_inc(get_sem(op.sig[0]), 16 if op.is_dma else 1)
                if e == "sp":
                    last = {}
                    for op in self.ops[e]:
                        if op.is_dma:
                            last[op.sig[0]] = max(last.get(op.sig[0], 0), op.sig[1])
                    for sname, v in last.items():
                        eng.wait_ge(get_sem(sname), v)
            return body

        for e in self.ENGS:
            if self.ops[e]:
                dec[e](make(e))


def build_program(depth=DEPTH):
    nc = bass.Bass("TRN2", target_bir_lowering=False)
    S = Sched()

    def din(name, shape):
        return nc.dram_tensor(name, list(shape), F32, kind="ExternalInput").ap()

    def dout(name, shape):
        return nc.dram_tensor(name, list(shape), F32, kind="ExternalOutput").ap()

    x_d = din("x", (NT, D))
    cond_d = din("condT", (128, 8))
    ck_d = {"diff": din("ck_diff", (2, L_CTX, D)), "na": din("ck_na", (2, L_CTX, D))}
    cv_d = {"diff": din("cv_diff", (2, L_CTX, D)), "na": din("cv_na", (2, L_CTX, D))}
    w_ada_d = din("w_ada", (DEPTH, D, 6 * D))
    b_ada_d = din("b_adaT", (DEPTH, 128, 48))
    gmix_d = din("g_mixT", (DEPTH, 128, 8))
    gmlp_d = din("g_mlpT", (DEPTH, 128, 8))
    w_fc1_d = din("w_fc1", (DEPTH, D, 4 * D))
    w_fc2_d = din("w_fc2", (DEPTH, 4 * D, D))
    w_qkv_d = {"diff": din("w_qkv_diff", (2, D, 3 * D)), "na": din("w_qkv_na", (2, D, 3 * D))}
    w_o_d = {"diff": din("w_o_diff", (2, D, D)), "na": din("w_o_na", (2, D, D))}
    qkg_d = din("qkg", (128, 8))
    subg_d = din("subg", (128, 2))
    lamv_d = din("lamv", (64, 8))
    ident_d = din("ident", (128, 128))
    ones_d = din("ones", (128, 128))
    bones_d = din("blockones", (128, 128))
    rrope_d = din("rropeT", (128, 128))
    ecomb_d = din("ecomb", (62, 191))
    colmask_d = din("colmask", (128, 64))
    kind_d = din("kind", (64, NKEY))
    qind_d = {"diff": din("qind_diff", (64, NT)), "na": din("qind_na", (64, NT))}
    cos_d = din("cosT", (128, NT))
    sin_d = din("sinT", (128, NT))
    relt2_d = din("relT2", (2, 62, 240))

    y_d = dout("y", (NT, D))
    nk_d = {"diff": dout("nk_diff", (2, NT, D)), "na": dout("nk_na", (2, NT, D))}
    nv_d = {"diff": dout("nv_diff", (2, NT, D)), "na": dout("nv_na", (2, NT, D))}

    es = ExitStack()
    with es:
        def sb(name, shape, dt):
            return es.enter_context(nc.sbuf_tensor(name, list(shape), dt))

        xT = sb("xT", (128, 8, NT), F32)
        hT = sb("hT", (128, 8, NT), BF16)
        OT = sb("OT", (128, 8, NT), BF16)
        slabs = [sb("slab%d" % i, (128, 4096), BF16) for i in range(3)]
        PTP = [sb("PTP%d" % i, (128, 1024), BF16) for i in range(2)]
        TW = [sb("TW%d" % i, (128, 1024), F32) for i in range(3)]
        TH = [sb("TH%d" % i, (128, 512), F32) for i in range(8)]
        TH2 = [TW[0][:, 0:512], TW[0][:, 512:1024], TW[1][:, 0:512], TW[1][:, 512:1024]]
        TW2b = TW[2].bitcast(BF16)
        TB2 = [TW2b[:, 0:512], TW2b[:, 512:1024]]
        modT = [sb("modT%d" % i, (128, 48), F32) for i in range(2)]
        modv = [sb("modv%d" % i, (128, 16), F32) for i in range(2)]
        badaT = sb("badaT", (128, 48), F32)
        gmixT = sb("gmixT", (128, 8), F32)
        gmlpT = sb("gmlpT", (128, 8), F32)
        condT = sb("condT_sb", (128, 8), F32)
        scb = sb("scb", (128, 8), BF16)
        qkg = sb("qkg_sb", (128, 8), F32)
        subg = sb("subg_sb", (128, 2), F32)
        subg2 = sb("subg2_sb", (128, 2), F32)
        lamv = sb("lamv_sb", (64, 8), F32)
        lamp = sb("lamp_sb", (64, 2), F32)
        lame = sb("lame_sb", (128, 2), F32)
        neglam = sb("neglam_sb", (128, 1), F32)
        ident = sb("ident_sb", (128, 128), F32)
        identb = sb("identb_sb", (128, 128), BF16)
        onesf = sb("ones_sb", (128, 128), F32)
        onesb = sb("onesb_sb", (128, 128), BF16)
        bonesb = sb("bonesb_sb", (128, 128), BF16)
        rropeb = sb("rropeb_sb", (128, 128), BF16)
        TB = [sb("TB%d" % i, (128, 512), BF16) for i in range(3)]
        ecombb = sb("ecombb_sb", (62, 191), BF16)
        relhi = sb("relhi_sb", (62, 120), BF16)
        rello = sb("rello_sb", (62, 120), BF16)
        colmask = sb("colmask_sb", (128, 64), F32)
        relt2 = sb("relt2_sb", (62, 120), F32)
        arena = sb("arena", (128, 40960), BF16)
        arena32 = arena.bitcast(F32)
        off = 0
        qt = [[None, None], [None, None]]
        kt = [[None, None], [None, None]]
        for b in range(2):
            for s in range(2):
                qt[b][s] = arena[:, off:off + NT]
                off += NT
                kt[b][s] = arena[:, off:off + NKEY]
                off += NKEY
        Vb = arena[:, off:off + 10 * D].rearrange("p (t e) -> p t e", t=10)
        off += 10 * D
        ctok = arena[:, off:off + 2 * D].rearrange("p (t e) -> p t e", t=2)
        off += 2 * D
        assert off % 2 == 0
        off32 = off // 2
        GT = arena32[:, off32:off32 + 8 * 15 * 64]
        cosT = arena32[:, off32:off32 + NT]
        sinT = arena32[:, off32 + NT:off32 + 2 * NT]
        off32 += 8 * 15 * 64
        kst = [arena32[:, off32 + i * 512:off32 + (i + 1) * 512] for i in range(2)]
        off32 += 1024
        vst = [arena32[:, off32 + i * 512:off32 + (i + 1) * 512] for i in range(2)]
        off32 += 1024
        assert off32 * 2 <= 40960, off32
        h1T = arena[:, 0:32 * NT].rearrange("p (c t) -> p c t", c=32)
        ARENA_KEYS = []

        SS = es.enter_context(nc.psum_tensor("bankS", [128, 1024], F32))
        banks = [SS[:, 0:512], SS[:, 512:1024]] + \
            [es.enter_context(nc.psum_tensor("bank%d" % i, [128, 512], F32)) for i in range(2, 8)]

        def PS(b):
            return ("ps", b)

        def dma_sp(out, in_, reads=(), writes=()):
            return S.add("sp", lambda e, out=out, in_=in_: e.dma_start(out=out, in_=in_), reads, writes, dma=True)

        def dma_cast(out, in_, reads=(), writes=()):
            return S.add("pool", lambda e, out=out, in_=in_: e.dma_start(out=out, in_=in_), reads, writes, dma=True)

        def mm(out, lhsT, rhs, start, stop, reads, writes):
            return S.add("pe", lambda e, out=out, lhsT=lhsT, rhs=rhs, start=start, stop=stop:
                         e.matmul(out, lhsT, rhs, start=start, stop=stop), reads, writes)

        def tr(out, in_, idt, reads, writes):
            return S.add("pe", lambda e, out=out, in_=in_, idt=idt: e.transpose(out, in_, idt), reads, writes)

        def act(out, in_, func, reads, writes, bias=None, scale=None):
            kw = {}
            if bias is not None:
                kw["bias"] = bias
            if scale is not None:
                kw["scale"] = scale
            return S.add("act", lambda e, out=out, in_=in_, func=func, kw=kw:
                         e.activation(out=out, in_=in_, func=func, **kw), reads, writes)

        def stt(out, in0, scalar, in1, op0, op1, reads, writes):
            return S.add("dve", lambda e, out=out, in0=in0, scalar=scalar, in1=in1, op0=op0, op1=op1:
                         e.scalar_tensor_tensor(out=out, in0=in0, scalar=scalar, in1=in1, op0=op0, op1=op1),
                         reads, writes)

        def tt(out, in0, in1, op, reads, writes):
            return S.add("dve", lambda e, out=out, in0=in0, in1=in1, op=op:
                         e.tensor_tensor(out=out, in0=in0, in1=in1, op=op), reads, writes)

        def ts(out, in0, s1, s2, op0, op1, reads, writes):
            if s2 is None:
                return S.add("dve", lambda e, out=out, in0=in0, s1=s1, op0=op0:
                             e.tensor_scalar(out, in0, s1, None, op0=op0), reads, writes)
            return S.add("dve", lambda e, out=out, in0=in0, s1=s1, s2=s2, op0=op0, op1=op1:
                         e.tensor_scalar(out, in0, s1, s2, op0=op0, op1=op1), reads, writes)

        def vcopy(out, in_, reads, writes):
            return S.add("dve", lambda e, out=out, in_=in_: e.tensor_copy(out=out, in_=in_), reads, writes)

        def ptt(out, in0, in1, op, reads, writes):
            return S.add(POOL_ENG, lambda e, out=out, in0=in0, in1=in1, op=op:
                         e.tensor_tensor(out=out, in0=in0, in1=in1, op=op), reads, writes)

        def pcopy(out, in_, reads, writes):
            return S.add(POOL_ENG, lambda e, out=out, in_=in_: e.tensor_copy(out=out, in_=in_), reads, writes)

        def recip(out, in_, reads, writes):
            return S.add("dve", lambda e, out=out, in_=in_: e.reciprocal(out=out, in_=in_), reads, writes)

        slab_ctr = [0]

        def next_slab():
            i = slab_ctr[0] % 3
            slab_ctr[0] += 1
            return i

        def load_slab_k8(w2d, col0, width, col_off=0, si=None):
            if si is None:
                si = next_slab()
            return si

        C = "const"
        dma_sp(ident[:], ident_d, (), [("c", "ident")])
        dma_cast(identb[:], ident_d, (), [("c", "identb")])
        dma_sp(onesf[:], ones_d, (), [("c", "ones")])
        dma_cast(onesb[:], ones_d, (), [("c", "onesb")])
        dma_cast(bonesb[:], bones_d, (), [("c", "bones")])
        dma_cast(rropeb[:], rrope_d, (), [("c", "rrope")])
        dma_cast(ecombb[:], ecomb_d, (), [("c", "ecomb")])
        dma_sp(colmask[:], colmask_d, (), [("c", "colmask")])
        dma_sp(condT[:], cond_d, (), [("c", "cond")])
        dma_sp(qkg[:], qkg_d, (), [("c", "qkg")])
        dma_sp(subg[:], subg_d, (), [("c", "subg")])
        dma_sp(lamv[:], lamv_d, (), [("c", "lamv")])
        act(scb[:], condT[:], AF.Silu, [("c", "cond")], [("c", "scb")])
        for b in range(2):
            dma_cast(kt[b][0][64:128, :], kind_d, (), [("kt", b, 0, "ind")])
            dma_cast(kt[b][1][0:64, :], kind_d, (), [("kt", b, 1, "ind")])

        for t in range(8):
            tw = TW[t % 2]
            twk = ("TW", t % 2)
            dma_sp(tw[:], x_d[t * 128:(t + 1) * 128, :], (), [twk])
            for g in range(2):
                bk = (2 * t + g) % 4
                for cc in range(4):
                    c = g * 4 + cc
                    tr(banks[bk][:, cc * 128:(cc + 1) * 128], tw[:, c * 128:(c + 1) * 128], ident[:],
                       [twk, ("c", "ident")], [PS(bk)])
                S.add("act", lambda e, bk=bk, g=g, t=t: e.activation(
                    out=xT[:, g * 4:(g + 1) * 4, t * 128:(t + 1) * 128],
                    in_=banks[bk][:, :].rearrange("p (c n) -> p c n", c=4), func=AF.Copy),
                    [PS(bk)], [("xT", c2) for c2 in range(g * 4, g * 4 + 4)])

        def ada_steps(l):
            steps = []
            par = l % 2
            for s in range(12):
                def step(s=s):
                    if s == 0:
                        dma_sp(badaT[:], b_ada_d[l], (), [("c", "bada")])
                    si = next_slab()
                    sl = slabs[si]
                    dma_cast(sl[:, 0:4096].rearrange("p (k n) -> p k n", k=8),
                             w_ada_d[l][:, s * 512:(s + 1) * 512].rearrange("(k p) n -> p k n", p=128),
                             (), [("slab", si)])
                    for m in range(4):
                        col = s * 4 + m
                        for kc in range(8):
                            mm(banks[6][:, col:col + 1], sl[:, kc * 512 + m * 128: kc * 512 + (m + 1) * 128],
                               scb[:, kc:kc + 1], kc == 0, kc == 7,
                               [("slab", si), ("c", "scb")], [PS(6)])
                    tt(modT[par][:, s * 4:(s + 1) * 4], banks[6][:, s * 4:(s + 1) * 4], badaT[:, s * 4:(s + 1) * 4],
                       ALU.add, [PS(6), ("c", "bada")], [("mod", par, s)])
                steps.append(step)
            return steps

        def mod_derive(l, which):
            par = l % 2
            if which == 0:
                dma_sp(gmixT[:], gmix_d[l], (), [("c", "gmix")])
                stt(modv[par][:, 0:8], modT[par][:, 8:16], 1.0, gmixT[:], ALU.add, ALU.mult,
                    [("mod", par, 2), ("mod", par, 3), ("c", "gmix")], [("modv", par, 0)])
            else:
                dma_sp(gmlpT[:], gmlp_d[l], (), [("c", "gmlp")])
                stt(modv[par][:, 8:16], modT[par][:, 32:40], 1.0, gmlpT[:], ALU.add, ALU.mult,
                    [("mod", par, 8), ("mod", par, 9), ("c", "gmlp")], [("modv", par, 1)])

        ada_queue = []

        def pump_ada(n=1):
            for _ in range(n):
                if ada_queue:
                    ada_queue.pop(0)()

        def norm_sq(c):
            for hf in range(2):
                act(TB[hf][:], xT[:, c, hf * 512:(hf + 1) * 512], AF.Square, [("xT", c)], [("TB", hf)])

        def norm_ss(c):
            for hf in range(2):
                mm(banks[4 + hf][:], onesb[:], TB[hf][:], c == 0, c == 7,
                   [("TB", hf), ("c", "onesb")], [PS(4 + hf)])

        def norm_stats(c):
            norm_sq(c)
            norm_ss(c)

        def norm_stats_delayed(m):
            if m > 0:
                norm_ss(m - 1)
            norm_sq(m)
            if m == 7:
                norm_ss(7)

        def norm_mod(l, which, have_stats=False):
            par = l % 2
            gs = modv[par][:, 0:8] if which == 0 else modv[par][:, 8:16]
            shift = modT[par][:, 0:8] if which == 0 else modT[par][:, 24:32]
            mk = [("mod", par, 0 + 6 * which), ("mod", par, 1 + 6 * which), ("modv", par, which)]
            if not have_stats:
                for c in range(8):
                    norm_stats(c)
            rs = TW[2]
            for hf in range(2):
                act(rs[:, hf * 512:(hf + 1) * 512], banks[4 + hf][:], AF.Ln, [PS(4 + hf)], [("TW", 2, hf)],
                    bias=EPS, scale=1.0 / D)
                act(rs[:, hf * 512:(hf + 1) * 512], rs[:, hf * 512:(hf + 1) * 512], AF.Exp, [("TW", 2, hf)],
                    [("TW", 2, hf)], scale=-0.5)
            for c in range(8):
                tw = TW[c % 2]
                twk = ("TW", c % 2)
                tt(tw[:], xT[:, c, :], rs[:], ALU.mult, [("xT", c), ("TW", 2, 0), ("TW", 2, 1)], [twk])
                act(hT[:, c, :], tw[:], AF.Identity, [twk] + mk, [("hT", c)], bias=shift[:, c:c + 1],
                    scale=gs[:, c:c + 1])

        def attention_layer(l):
            T = "diff" if l % 2 == 0 else "na"
            i = l // 2
            par = l % 2
            wq = w_qkv_d[T][i]
            lam_init = 0.8 - 0.6 * math.exp(-0.3 * l)
            gate = modT[par][:, 16:24]
            qg = qkg[:, (0 if T == "diff" else 4) + i:(0 if T == "diff" else 4) + i + 1]
            kg = qkg[:, (2 if T == "diff" else 6) + i:(2 if T == "diff" else 6) + i + 1]

            norm_mod(l, 0, have_stats=(l > 0))
            _stage('norm')

            for b in range(2):
                dma_cast(qt[b][0][64:128, :], qind_d[T], (), [("qt", b, 0, "ind")])
                dma_cast(qt[b][1][0:64, :], qind_d[T], (), [("qt", b, 1, "ind")])
            dma_cast(ctok[:, :, :], ck_d[T][i].rearrange("(t p) e -> p t e", p=128), (), [("ctok",)])
            dma_cast(Vb[:, 8:10, :], cv_d[T][i].rearrange("(t p) e -> p t e", p=128), (), [("V", 8), ("V", 9)])

            if T == "diff":
                dma_sp(cosT, cos_d, (), [("GT",)])
                dma_sp(sinT, sin_d, (), [("GT",)])
                tt(lamp[:, 0:1], lamv[:, i * 4:i * 4 + 1], lamv[:, i * 4 + 1:i * 4 + 2], ALU.mult,
                   [("c", "lamv")], [("lamp",)])
                tt(lamp[:, 1:2], lamv[:, i * 4 + 2:i * 4 + 3], lamv[:, i * 4 + 3:i * 4 + 4], ALU.mult,
                   [("c", "lamv"), ("lamp",)], [("lamp",)])
                mm(banks[7][:, 0:2], onesf[0:64, :], lamp[:, :], True, True, [("lamp",), ("c", "ones")], [PS(7)])
                act(lame[:], banks[7][:, 0:2], AF.Exp, [PS(7)], [("lame",)])
                stt(neglam[:], lame[:, 1:2], -lam_init, lame[:, 0:1], ALU.add, ALU.subtract,
                    [("lame",)], [("neglam",)])
                ts(subg2[:, i:i + 1], subg[:, i:i + 1], 1.0 - lam_init, None, ALU.mult, None,
                   [("c", "subg")], [("subg2", i)])

            _stage('pre')
            def v_projection():
                vctr = 0
                for s in range(2):
                    si = next_slab()
                    sl = slabs[si]
                    dma_cast(sl[:, 0:4096].rearrange("p (k n) -> p k n", k=8),
                             wq[:, 2048 + s * 512:2048 + (s + 1) * 512].rearrange("(k p) n -> p k n", p=128),
                             (), [("slab", si)])
                    for t in range(8):
                        bk = vctr % 4
                        for kc in range(8):
                            mm(banks[bk][:], hT[:, kc, t * 128:(t + 1) * 128], sl[:, kc * 512:(kc + 1) * 512],
                               kc == 0, kc == 7, [("slab", si), ("hT", kc)], [PS(bk)])
                        st = vst[vctr % 2]
                        stk = ("vst", vctr % 2)
                        act(st, banks[bk][:], AF.Copy, [PS(bk)], [stk])
                        vcopy(Vb[:, t, s * 512:(s + 1) * 512], banks[bk][:], [PS(bk)], [("V", t)])
                        dma_sp(nv_d[T][i][t * 128:(t + 1) * 128, s * 512:(s + 1) * 512], st, [stk], ())
                        vctr += 1
                        for _ in range(2):
                            step(bg_lo)
                            step(bg_k)

            _stage('vproj')
            def build_gtable(hg):
                dma_sp(relt2[:], relt2_d[i][:, hg * 120:(hg + 1) * 120], (), [("c", "relt2")])
                vcopy(relhi[:], relt2[:], [("c", "relt2")], [("c", "relhi")])
                tt(rello[:], relt2[:], relhi[:], ALU.subtract, [("c", "relt2"), ("c", "relhi")], [("c", "rello")])
                for qc in range(64):
                    bk = 6 + (qc % 2)
                    mm(banks[bk][:, 0:120], ecombb[:, 63 - qc:191 - qc], relhi[:, :],
                       True, False, [("c", "ecomb"), ("c", "relhi")], [PS(bk)])
                    mm(banks[bk][:, 0:120], ecombb[:, 63 - qc:191 - qc], rello[:, :],
                       False, True, [("c", "ecomb"), ("c", "rello")], [PS(bk)])
                    S.add("dve", lambda e, bk=bk, qc=qc: e.tensor_scalar(
                        GT.rearrange("p (a q) -> p a q", q=64)[:, :, qc], banks[bk][:, 0:120],
                        8.0, colmask[:, qc:qc + 1], op0=ALU.mult, op1=ALU.add),
                        [PS(bk), ("c", "colmask")], [("GT",)])
                for q4 in range(4):
                    act(GT[:, q4 * 1920:(q4 + 1) * 1920], GT[:, q4 * 1920:(q4 + 1) * 1920], AF.Exp,
                        [("GT",)], [("GT",)], scale=0.125)

            def prep_load(c):
                si = next_slab()
                sl = slabs[si]
                dma_cast(sl[:, 0:1024].rearrange("p (k n) -> p k n", k=8),
                         wq[:, c * 128:(c + 1) * 128].rearrange("(k p) n -> p k n", p=128),
                         (), [("slab", si)])
                dma_cast(sl[:, 1024:2048].rearrange("p (k n) -> p k n", k=8),
                         wq[:, 1024 + c * 128:1024 + (c + 1) * 128].rearrange("(k p) n -> p k n", p=128),
                         (), [("slab", si)])
                return si

            def prep(c, X, si):
                b = c % 2
                sl = slabs[si]
                bk = 6 + X
                if X == 0:
                    tq, tsq, trs, tn = TH[0], TH[1], TH[2], TH[3]
                    tb0, tb1 = TB[0], TB[1]
                    kq, ksq, krs, kn, kb0, kb1 = ("TH", 0), ("TH", 1), ("TH", 2), ("TH", 3), ("TB", 0), ("TB", 1)
                else:
                    tq, tsq, trs, tn = TH2
                    tb0, tb1 = TB2[0], TB2[1]
                    kq, ksq, krs, kn, kb0, kb1 = ("TH2", 0), ("TH2", 1), ("TH2", 2), ("TH2", 3), ("TB2", 0), ("TB2", 1)
                    for j in range(2):
                        mm(banks[bk][:, j * 128:(j + 1) * 128], ctok[:, j, c * 128:(c + 1) * 128], identb[:],
                           True, True, [("ctok",), ("c", "identb")], [PS(bk)])
                    yield
                    vcopy(kt[b][0][0:64, NT:NKEY], banks[bk][0:64, 0:256], [PS(bk)], [("kt", b, 0, "ctx")])
                    vcopy(kt[b][1][64:128, NT:NKEY], banks[bk][64:128, 0:256], [PS(bk)],
                          [("kt", b, 1, "ctx")])
                    yield
                g = qg if X == 0 else kg
                dst = qt if X == 0 else kt
                nm = "qt" if X == 0 else "kt"
                for hf in range(2):
                    cs = slice(hf * 512, (hf + 1) * 512)
                    for kc in range(8):
                        mm(banks[bk][:], sl[:, X * 1024 + kc * 128:X * 1024 + (kc + 1) * 128], hT[:, kc, cs],
                           kc == 0, kc == 7, [("slab", si), ("hT", kc)], [PS(bk)])
                        if kc % 2 == 1:
                            yield
                    vcopy(tq[:], banks[bk][:], [PS(bk)], [kq])
                    yield
                    tt(tb0[:], tq[:], tq[:], ALU.mult, [kq], [kb0])
                    yield
                    mm(banks[bk][:], bonesb[:], tb0[:], True, True, [kb0, ("c", "bones")], [PS(bk)])
                    yield
                    act(trs[:], banks[bk][:], AF.Ln, [PS(bk)], [krs], bias=EPS, scale=1.0 / 64)
                    yield
                    act(trs[:], trs[:], AF.Exp, [krs], [krs], scale=-0.5)
                    yield
                    stt(tn[:], tq[:], g, trs[:], ALU.mult, ALU.mult, [kq, krs, ("c", "qkg")], [kn])
                    fin = tn
                    fink = kn
                    yield
                    if T == "diff":
                        vcopy(tb1[:], tn[:], [kn], [kb1])
                        tt(tq[:], tn[:], cosT[:, cs], ALU.mult, [kn, ("GT",)], [kq])
                        yield
                        mm(banks[bk][:], rropeb[:], tb1[:], True, True, [kb1, ("c", "rrope")], [PS(bk)])
                        yield
                        tt(tsq[:], banks[bk][:], sinT[:, cs], ALU.mult, [PS(bk), ("GT",)], [ksq])
                        yield
                        tt(trs[:], tq[:], tsq[:], ALU.add, [kq, ksq], [krs])
                        fin = trs
                        fink = krs
                        yield
                    vcopy(dst[b][0][0:64, cs], fin[0:64, :], [fink], [(nm, b, 0, "lat", hf)])
                    vcopy(dst[b][1][64:128, cs], fin[64:128, :], [fink], [(nm, b, 1, "lat", hf)])
                    if X == 1:
                        for tb in range(4):
                            tr(banks[bk][:, tb * 128:(tb + 1) * 128], fin[:, tb * 128:(tb + 1) * 128], ident[:],
                               [fink, ("c", "ident")], [PS(bk)])
                        yield
                        ks = kst[hf]
                        vcopy(ks, banks[bk][:], [PS(bk)], [("kst", hf)])
                        dma_sp(nk_d[T][i][hf * 512:(hf + 1) * 512, c * 128:(c + 1) * 128]
                               .rearrange("(t p) e -> p t e", p=128),
                               ks.rearrange("p (t e) -> p t e", t=4), [("kst", hf)], ())
                    yield

            def epilogue(c, hq, s, bO, bZ):
                qs = slice(hq * 512, (hq + 1) * 512)
                rz = TH[4]
                act(rz[:], banks[bZ][:], AF.Ln, [PS(bZ)], [("TH", 4)])
                yield
                act(rz[:], rz[:], AF.Exp, [("TH", 4)], [("TH", 4)], scale=-1.0)
                yield
                if T == "na":
                    ps_ = slice(0, 64) if s == 0 else slice(64, 128)
                    tt(OT[ps_, c, qs], banks[bO][ps_, :], rz[ps_, :], ALU.mult, [PS(bO), ("TH", 4)],
                       [("OT", c, hq, s)])
                    return
                if s == 0:
                    tt(TH[5][:], banks[bO][:], rz[:], ALU.mult, [PS(bO), ("TH", 4)], [("TH", 5)])
                    return
                tt(TH[6][:], banks[bO][:], rz[:], ALU.mult, [PS(bO), ("TH", 4)], [("TH", 6)])
                yield
                stt(TH[7][:], TH[6][:], neglam[:, 0:1], TH[5][:], ALU.mult, ALU.add,
                    [("TH", 5), ("TH", 6), ("neglam",)], [("TH", 7)])
                yield
                tt(TB[2][:], TH[7][:], TH[7][:], ALU.mult, [("TH", 7)], [("TB", 2)])
                yield
                mm(banks[bZ][:], onesb[:], TB[2][:], True, True, [("TB", 2), ("c", "onesb")], [PS(bZ)])
                yield
                act(TH[5][:], banks[bZ][:], AF.Ln, [PS(bZ)], [("TH", 5)], bias=EPS, scale=1.0 / 128)
                yield
                act(TH[5][:], TH[5][:], AF.Exp, [("TH", 5)], [("TH", 5)], scale=-0.5)
                yield
                stt(OT[:, c, qs], TH[7][:], subg2[:, i:i + 1], TH[5][:], ALU.mult, ALU.mult,
                    [("TH", 7), ("TH", 5), ("subg2", i)], [("OT", c, hq, 0), ("OT", c, hq, 1)])

            bg_hi = []
            bg_lo = []

            def step(q):
                while q:
                    try:
                        next(q[0])
                        return True
                    except StopIteration:
                        q.pop(0)
                return False

            def drain(q, keep=0):
                while len(q) > keep:
                    g0 = q[0]
                    for _ in g0:
                        pass
                    q.pop(0)

            bg_k = []

            def pump():
                step(bg_hi)
                step(bg_lo)
                step(bg_k)

            pt_ctr = [0]
            st_ctr = [0]
            acc_ctr = [0]

            streams = {}
            seq = []
            for c in range(8):
                for hq in range(2):
                    if T == "diff":
                        blocks = list(range(10))
                    else:
                        blocks = ([0, 1, 2, 3, 4, 5] if hq == 0 else [2, 3, 4, 5, 6, 7]) + [8, 9]
                    for s in range(2):
                        k = (c, hq, s)
                        streams[k] = blocks
                        seq += [(k, p) for p in range(len(blocks) // 2)]
            pis = {}
            acc = {}
            started = set()

            def finish_prep():
                while bg_lo or bg_k:
                    step(bg_lo)
                    step(bg_k)

            def ensure_chunk(c):
                if c in started:
                    return
                started.add(c)
                finish_prep()
                if T == "na" and c == 4:
                    build_gtable(1)
                if c + 1 < 8:
                    si1 = prep_load(c + 1)
                    bg_lo.append(prep(c + 1, 0, si1))
                    bg_k.append(prep(c + 1, 1, si1))

            def issue_s_pair(k, p):
                c, hq, s = k
                ensure_chunk(c)
                b = c % 2
                blocks = streams[k]
                qs = slice(hq * 512, (hq + 1) * 512)
                gmul = []
                for h in range(2):
                    j = blocks[2 * p + h]
                    kr = [("kt", b, s, "ind"), ("qt", b, s, "ind"), ("qt", b, s, "lat", hq)]
                    kr.append(("kt", b, s, "ctx") if j >= 8 else ("kt", b, s, "lat", j // 4))
                    mm(banks[h], kt[b][s][:, j * 128:(j + 1) * 128], qt[b][s][:, qs], True, True,
                       kr, [PS(h)])
                    if T == "na" and j < 8:
                        a = 2 * j
                        lo, hi = (0, a + 5) if a <= 6 else (a - 3, 15)
                        lo = max(lo, 8 * hq)
                        hi = min(hi, 8 * hq + 7)
                        hl = (2 * c + s) % 8
                        sp0 = 7 - a + lo
                        n = (hi - lo + 1) * 64
                        go = (hl * 15 + sp0) * 64
                        po = h * 512 + (lo - 8 * hq) * 64
                        gmul.append((po, n, go))
                pi = pt_ctr[0] % 2
                pt_ctr[0] += 1
                act(PTP[pi][:], SS[:, :], AF.Exp, [PS(0), PS(1)], [("PT", pi)], scale=0.125)
                for po, n, go in gmul:
                    tt(PTP[pi][:, po:po + n], PTP[pi][:, po:po + n], GT[:, go:go + n], ALU.mult,
                       [("PT", pi), ("GT",)], [("PT", pi)])
                pis[(k, p)] = pi

            def issue_av_pair(k, p):
                c, hq, s = k
                blocks = streams[k]
                nb = len(blocks)
                if p == 0:
                    drain(bg_hi, keep=1)
                    aset = acc_ctr[0] % 2
                    acc_ctr[0] += 1
                    acc[k] = (2 + 2 * aset, 3 + 2 * aset)
                bO, bZ = acc[k]
                pi = pis[(k, p)]
                for h in range(2):
                    bi = 2 * p + h
                    j = blocks[bi]
                    rhs = PTP[pi][:, h * 512:(h + 1) * 512]
                    mm(banks[bO][:], Vb[:, j, c * 128:(c + 1) * 128], rhs, bi == 0, bi == nb - 1,
                       [("PT", pi), ("V", j)], [PS(bO)])
                    mm(banks[bZ][:], onesb[:], rhs, bi == 0, bi == nb - 1,
                       [("PT", pi), ("c", "onesb")], [PS(bZ)])
                if p == nb // 2 - 1:
                    bg_hi.append(epilogue(c, hq, s, bO, bZ))

            if T == "na":
                build_gtable(0)
            arena_fence(False)
            si0 = prep_load(0)
            bg_lo.append(prep(0, 0, si0))
            bg_k.append(prep(0, 1, si0))
            v_projection()
            issue_s_pair(*seq[0])
            for g, (k, p) in enumerate(seq):
                if g + 1 < len(seq):
                    issue_s_pair(*seq[g + 1])
                issue_av_pair(k, p)
                pump()
                pump()
            finish_prep()
            drain(bg_hi)

            _stage('attn')
            if l == 0:
                pump_ada(2)
            octr = 0
            for s in range(2):
                si = next_slab()
                sl = slabs[si]
                dma_cast(sl[:, 0:4096].rearrange("p (k n) -> p k n", k=8),
                         w_o_d[T][i][:, s * 512:(s + 1) * 512].rearrange("(k p) n -> p k n", p=128),
                         (), [("slab", si)])
                for m4 in range(4):
                    m = s * 4 + m4
                    for hf in range(2):
                        bk = octr % 4
                        octr += 1
                        cs = slice(hf * 512, (hf + 1) * 512)
                        for kc in range(8):
                            mm(banks[bk][:], sl[:, kc * 512 + m4 * 128:kc * 512 + (m4 + 1) * 128], OT[:, kc, cs],
                               kc == 0, kc == 7,
                               [("slab", si), ("OT", kc, hf, 0), ("OT", kc, hf, 1)], [PS(bk)])
                        stt(xT[:, m, cs], banks[bk][:], gate[:, m:m + 1], xT[:, m, cs], ALU.mult, ALU.add,
                            [PS(bk), ("xT", m), ("mod", par, 4), ("mod", par, 5)], [("xT", m)])
                    norm_stats_delayed(m)

        def mlp_layer(l):
            par = l % 2
            gate = modT[par][:, 40:48]
            _stage('oproj')
            if l == 0:
                pump_ada(4)
            mod_derive(l, 1)
            norm_mod(l, 1, have_stats=True)
            _stage('norm2')
            ctr = 0
            for s in range(8):
                si = next_slab()
                sl = slabs[si]
                dma_cast(sl[:, 0:4096].rearrange("p (k n) -> p k n", k=8),
                         w_fc1_d[l][:, s * 512:(s + 1) * 512].rearrange("(k p) n -> p k n", p=128),
                         (), [("slab", si)])
                for m4 in range(4):
                    f = s * 4 + m4
                    for hf in range(2):
                        bk = ctr % 4
                        cs = slice(hf * 512, (hf + 1) * 512)
                        for kc in range(8):
                            mm(banks[bk][:], sl[:, kc * 512 + m4 * 128:kc * 512 + (m4 + 1) * 128], hT[:, kc, cs],
                               kc == 0, kc == 7, [("slab", si), ("hT", kc)], [PS(bk)])
                        th = TH[ctr % 4]
                        thk = ("TH", ctr % 4)
                        act(th[:], banks[bk][:], AF.Relu, [PS(bk)], [thk])
                        tt(h1T[:, f, cs], th[:], th[:], ALU.mult, [thk], [("h1T", f, hf)])
                        ctr += 1
                pump_ada(1)
            _stage('fc1')
            for m in range(8):
                si = next_slab()
                sl = slabs[si]
                dma_cast(sl[:, 0:4096].rearrange("p (k n) -> p k n", k=32),
                         w_fc2_d[l][:, m * 128:(m + 1) * 128].rearrange("(k p) n -> p k n", p=128),
                         (), [("slab", si)])
                for hf in range(2):
                    bk = ctr % 4
                    ctr += 1
                    cs = slice(hf * 512, (hf + 1) * 512)
                    for kc in range(32):
                        mm(banks[bk][:], sl[:, kc * 128:(kc + 1) * 128], h1T[:, kc, cs], kc == 0, kc == 31,
                           [("slab", si), ("h1T", kc, hf)], [PS(bk)])
                    stt(xT[:, m, cs], banks[bk][:], gate[:, m:m + 1], xT[:, m, cs], ALU.mult, ALU.add,
                        [PS(bk), ("xT", m), ("mod", par, 10), ("mod", par, 11)], [("xT", m)])
                if l + 1 < depth:
                    norm_stats_delayed(m)
                pump_ada(1)

        arena_att_keys = []
        for b in range(2):
            for s in range(2):
                arena_att_keys += [("qt", b, s, "ind"), ("qt", b, s, "lat", 0), ("qt", b, s, "lat", 1),
                                   ("kt", b, s, "ind"), ("kt", b, s, "ctx"), ("kt", b, s, "lat", 0),
                                   ("kt", b, s, "lat", 1)]
        arena_att_keys += [("V", t) for t in range(10)] + [("ctok",), ("GT",)]
        arena_att_keys += [("kst", 0), ("kst", 1), ("vst", 0), ("vst", 1)]
        arena_mlp_keys = [("h1T", f, hf) for f in range(32) for hf in range(2)]

        def arena_fence(to_mlp):
            keys = arena_att_keys + arena_mlp_keys + [("TW", 0), ("TW", 1), ("TW", 2, 0), ("TW", 2, 1)] + \
                [("TH2", j) for j in range(4)] + [("TB2", 0), ("TB2", 1)]
            S.add("dve", lambda e: e.tensor_copy(out=lamp[:, 0:1], in_=lamp[:, 0:1]), keys, keys + [("lamp",)])

        ada_queue.extend(ada_steps(0))
        pump_ada(4)
        try:
            for l in range(depth):
                mod_derive(l, 0)
                if l > 0:
                    arena_fence(False)
                    for b in range(2):
                        dma_cast(kt[b][0][64:128, :], kind_d, (), [("kt", b, 0, "ind")])
                        dma_cast(kt[b][1][0:64, :], kind_d, (), [("kt", b, 1, "ind")])
                attention_layer(l)
                arena_fence(True)
                if l + 1 < depth:
                    ada_queue.extend(ada_steps(l + 1))
                mlp_layer(l)
                pump_ada(len(ada_queue))
        except _Stop:
            pass

        for t in range(8):
            tw = TW[t % 2]
            twk = ("TW", t % 2)
            for g in range(2):
                bk = (2 * t + g) % 4
                for cc in range(4):
                    c = g * 4 + cc
                    tr(banks[bk][:, cc * 128:(cc + 1) * 128], xT[:, c, t * 128:(t + 1) * 128], ident[:],
                       [("xT", c), ("c", "ident")], [PS(bk)])
                act(tw[:, g * 512:(g + 1) * 512], banks[bk][:], AF.Copy, [PS(bk)], [twk])
            dma_sp(y_d[t * 128:(t + 1) * 128, :], tw[:], [twk], ())

        S.lower(nc, es)
    return nc


def _const_tables():
    ident = np.eye(128, dtype=np.float32)
    ones = np.ones((128, 128), np.float32)
    bones = np.zeros((128, 128), np.float32)
    bones[:64, :64] = 1.0
    bones[64:, 64:] = 1.0
    rr = np.zeros((128, 128), np.float32)
    for m in range(128):
        d = m % 64
        if (d % 32) < 16:
            rr[m + 16, m] = -1.0
        else:
            rr[m - 16, m] = 1.0
    ecomb = np.zeros((62, 191), np.float32)
    for j in range(31):
        ecomb[j, j + 48] = 1.0
        ecomb[31 + j, j + 112] = 1.0
    return ident, ones, bones, rr, ecomb


def _core_tables(is_sample):
    rows = np.arange(NT) // 64
    cols = np.arange(NT) % 64
    kind = np.zeros((64, NKEY), np.float32)
    kind[rows, np.arange(NT)] = 1.0
    kind[16, NT:] = 1.0
    kind[17, :] = 1.0

    def qind(win, ctxvis):
        q = np.zeros((64, NT), np.float32)
        q[0:16, :] = BIGM * win[:, rows]
        q[16, :] = BIGM * ctxvis
        q[17, :] = -BIGM
        return q

    r = np.arange(16)
    if is_sample:
        win_diff = np.ones((16, 16), np.float32)
        rs = np.clip(r - 4, 0, 8)
        win_na = ((r[:, None] >= rs[None, :]) & (r[:, None] < rs[None, :] + 8)).astype(np.float32)
        ctxvis = 1.0
        kc = np.arange(64)[:, None]
        qc = np.arange(64)[None, :]
        cs = np.clip(qc - 8, 0, 48)
        valid = (kc >= cs) & (kc < cs + 16)
        colmask = np.where(valid, 0.0, -BIGM).astype(np.float32)
        colmask = np.concatenate([colmask, colmask], axis=0)
        n_freq = 16
        freqs = (np.float32(10000.0) ** (-np.arange(n_freq, dtype=np.float32) / np.float32(n_freq))).astype(np.float32)
        ang_row = rows.astype(np.float32)[None, :] * freqs[:, None]
        ang_col = cols.astype(np.float32)[None, :] * freqs[:, None]
        ang = np.zeros((128, NT), np.float32)
        for p in range(128):
            d = p % 64
            ang[p] = ang_row[d % 16] if d < 32 else ang_col[d % 16]
        cosT = np.cos(ang).astype(np.float32)
        sinT = np.sin(ang).astype(np.float32)
    else:
        same = (r[:, None] // 4 == r[None, :] // 4).astype(np.float32)
        win_diff = same
        win_na = same
        ctxvis = 0.0
        colmask = np.zeros((128, 64), np.float32)
        cosT = np.ones((128, NT), np.float32)
        sinT = np.zeros((128, NT), np.float32)
    return kind, qind(win_diff, ctxvis), qind(win_na, ctxvis), colmask, cosT, sinT


_NC_CACHE = {}
_RETURN_IN_MAPS = False
_DEPTH_RUN = DEPTH


def kernel(x_prompt, x_sample, cache_diff_k, cache_diff_v, cache_na_k, cache_na_v, c, c_ctx,
           w_ada, b_ada, norm_mix_g, norm_mlp_g, w_fc1, w_fc2,
           w_qkv_diff, w_o_diff, q_norm_diff_g, k_norm_diff_g,
           lambda_q1, lambda_k1, lambda_q2, lambda_k2, subln_g,
           w_qkv_na, w_o_na, q_norm_na_g, k_norm_na_g, rel_bias_na):
    f = lambda a: np.ascontiguousarray(np.asarray(a, dtype=np.float32))
    x_prompt, x_sample = f(x_prompt), f(x_sample)
    cache_diff_k, cache_diff_v, cache_na_k, cache_na_v = map(f, (cache_diff_k, cache_diff_v, cache_na_k, cache_na_v))
    c, c_ctx = f(c), f(c_ctx)
    ident, ones, bones, rr, ecomb = _const_tables()

    def colT(v, ncol):
        return np.ascontiguousarray(np.asarray(v, np.float32).reshape(ncol, 128).T)

    b_adaT = np.stack([colT(b_ada[l], 48) for l in range(DEPTH)])
    g_mixT = np.stack([colT(norm_mix_g[l], 8) for l in range(DEPTH)])
    g_mlpT = np.stack([colT(norm_mlp_g[l], 8) for l in range(DEPTH)])
    rep2 = lambda v: np.concatenate([np.asarray(v, np.float32)] * 2)
    qkg = np.stack([rep2(q_norm_diff_g[0]), rep2(q_norm_diff_g[1]), rep2(k_norm_diff_g[0]), rep2(k_norm_diff_g[1]),
                    rep2(q_norm_na_g[0]), rep2(q_norm_na_g[1]), rep2(k_norm_na_g[0]), rep2(k_norm_na_g[1])], axis=1)
    subg = np.stack([np.asarray(subln_g[0], np.float32), np.asarray(subln_g[1], np.float32)], axis=1)
    lamv = np.stack([np.asarray(v[i], np.float32) for i in range(2)
                     for v in (lambda_q1, lambda_k1, lambda_q2, lambda_k2)], axis=1)
    rel = np.asarray(rel_bias_na, np.float32)
    relT2 = np.zeros((2, 62, 240), np.float32)
    for half in range(2):
        for sp in range(15):
            dr = 14 - sp + half
            if dr > 14:
                continue
            relT2[:, half * 31:(half + 1) * 31, sp::15] = np.transpose(rel[:, :, dr, :], (0, 2, 1))
    shared = {
        "w_ada": f(w_ada), "b_adaT": f(b_adaT), "g_mixT": f(g_mixT), "g_mlpT": f(g_mlpT),
        "w_fc1": f(w_fc1), "w_fc2": f(w_fc2), "w_qkv_diff": f(w_qkv_diff), "w_qkv_na": f(w_qkv_na),
        "w_o_diff": f(w_o_diff), "w_o_na": f(w_o_na), "qkg": f(qkg), "subg": f(subg), "lamv": f(lamv),
        "ident": ident, "ones": ones, "blockones": bones, "rropeT": rr, "ecomb": ecomb, "relT2": relT2,
    }
    tabs = {True: _core_tables(True), False: _core_tables(False)}
    zc = np.zeros((2, L_CTX, D), np.float32)
    in_maps = []
    for core in range(N_CORES):
        is_sample = core < 4
        kind, qd, qn, colmask, cosT, sinT = tabs[is_sample]
        m = dict(shared)
        m.update({"kind": kind, "qind_diff": qd, "qind_na": qn, "colmask": colmask, "cosT": cosT, "sinT": sinT})
        if is_sample:
            bi = core
            m["x"] = x_sample[bi]
            m["condT"] = colT(c[bi], 8)
            m["ck_diff"] = f(cache_diff_k[bi].reshape(2, L_CTX, D))
            m["cv_diff"] = f(cache_diff_v[bi].reshape(2, L_CTX, D))
            m["ck_na"] = f(cache_na_k[bi].reshape(2, L_CTX, D))
            m["cv_na"] = f(cache_na_v[bi].reshape(2, L_CTX, D))
        else:
            p0 = (core - 4) * 4
            m["x"] = f(x_prompt[p0:p0 + 4].reshape(NT, D))
            m["condT"] = colT(c_ctx, 8)
            m["ck_diff"] = zc
            m["cv_diff"] = zc
            m["ck_na"] = zc
            m["cv_na"] = zc
        in_maps.append(m)

    if _RETURN_IN_MAPS:
        return in_maps
    if "nc" not in _NC_CACHE:
        _NC_CACHE["nc"] = build_program(_DEPTH_RUN)
    nc = _NC_CACHE["nc"]
    res = run_bass_kernel_spmd(nc, in_maps, core_ids=list(range(N_CORES)))
    R = res.results
    ys = np.stack([R[b]["y"] for b in range(4)]).astype(np.float32)
    yp = np.concatenate([R[4 + g]["y"].reshape(4, 256, D) for g in range(4)], axis=0).astype(np.float32)

    def gather_new(name, H, E):
        out = np.concatenate([np.transpose(R[4 + g][name].reshape(2, 4, 256, D), (1, 0, 2, 3)) for g in range(4)],
                             axis=0)
        return np.ascontiguousarray(out.reshape(16, 2, 256, H, E)).astype(np.float32)

    return (yp, ys, gather_new("nk_diff", 8, 128), gather_new("nv_diff", 8, 128),
            gather_new("nk_na", 16, 64), gather_new("nv_na", 16, 64))
```

```python
import math
from contextlib import ExitStack

import numpy as np
import concourse.bass as bass
import concourse.mybir as mybir
from concourse.bass_utils import run_bass_kernel_spmd

F32 = mybir.dt.float32
BF16 = mybir.dt.bfloat16
AF = mybir.ActivationFunctionType
ALU = mybir.AluOpType

D = 1024
NT = 1024
DEPTH = 4
L_CTX = 256
NKEY = NT + L_CTX
EPS = 1e-6
BIGM = 32768.0
N_CORES = 8
EMBED_WAIT = True
POOL_ENG = "dve"
_STOP = None


class _Stop(Exception):
    pass


def _stage(name):
    if _STOP == name:
        raise _Stop()


class _Op:
    __slots__ = ("eng", "emit", "deps", "is_dma", "sig", "need_sig", "idx", "waits", "knows")

    def __init__(self, eng, emit, is_dma):
        self.eng = eng
        self.emit = emit
        self.deps = []
        self.is_dma = is_dma
        self.sig = None
        self.need_sig = is_dma
        self.idx = -1


class Sched:
    ENGS = ("pe", "act", "dve", "pool", "sp")
    EPOCH = 6000
    NDMA = {"sp": 12, "pool": 6}

    def __init__(self):
        self.ops = {e: [] for e in self.ENGS}
        self.last_w = {}
        self.readers = {}
        self.dma_hist = {"sp": [], "pool": []}
        self.order = []

    def add(self, eng, emit, reads=(), writes=(), dma=False):
        op = _Op(eng, emit, dma)
        deps = []
        for k in reads:
            w = self.last_w.get(k)
            if w is not None:
                deps.append(w)
            if k[0] == "ps":
                deps.extend(r for r in self.readers.get(k, ()) if r.eng != eng)
        for k in writes:
            w = self.last_w.get(k)
            if w is not None:
                deps.append(w)
            deps.extend(self.readers.get(k, ()))
        if dma:
            h = self.dma_hist[eng]
            n = self.NDMA[eng]
            if len(h) >= n:
                deps.append(h[-n])
            h.append(op)
        seen = set()
        for d in deps:
            if d is op or id(d) in seen:
                continue
            if d.eng == "pe" and eng == "pe" and not d.is_dma:
                continue
            seen.add(id(d))
            op.deps.append(d)
            d.need_sig = True
        for k in reads:
            self.readers.setdefault(k, []).append(op)
        for k in writes:
            self.last_w[k] = op
            self.readers[k] = []
        op.idx = len(self.ops[eng])
        self.ops[eng].append(op)
        self.order.append(op)
        return op

    def lower(self, nc, es):
        sem_cache = {}

        def get_sem(name):
            if name not in sem_cache:
                sem_cache[name] = es.enter_context(nc.semaphore(name))
            return sem_cache[name]

        for e in self.ENGS:
            cnt = 0
            dcnt = 0
            for op in self.ops[e]:
                if op.is_dma:
                    n = self.NDMA[e]
                    op.sig = ("d_%s_%d" % (e, dcnt % n), 16 * (dcnt // n + 1))
                    dcnt += 1
                elif op.need_sig:
                    op.sig = ("c_%s_%d" % (e, cnt // self.EPOCH), cnt % self.EPOCH + 1)
                    cnt += 1
        for e in self.ENGS:
            for op in self.ops[e]:
                if op.sig is not None:
                    get_sem(op.sig[0])
        block = es.enter_context(nc.Block())
        dec = {"pe": block.tensor, "act": block.scalar, "dve": block.vector, "pool": block.gpsimd,
               "sp": block.sync}

        known = {e: {} for e in self.ENGS}
        for op in self.order:
            K = known[op.eng]
            need = {}
            for d in op.deps:
                sname, v = d.sig
                if K.get(sname, 0) >= v:
                    continue
                if need.get(sname, 0) < v:
                    need[sname] = v
            for d in op.deps:
                for s2, v2 in d.knows.items():
                    if K.get(s2, 0) < v2:
                        K[s2] = v2
            for sname, v in need.items():
                if K.get(sname, 0) < v:
                    K[sname] = v
            op.waits = list(need.items())
            op.knows = dict(K)
            if op.sig is not None:
                op.knows[op.sig[0]] = max(op.knows.get(op.sig[0], 0), op.sig[1])

        def make(e):
            def body(eng):
                for op in self.ops[e]:
                    items = list(op.waits)
                    emb = None
                    if EMBED_WAIT and items:
                        emb = items.pop()
                    for sname, v in items:
                        eng.wait_ge(get_sem(sname), v)
                    ins = op.emit(eng)
                    if emb is not None:
                        ins._wait_ge(get_sem(emb[0]), emb[1])
                    if op.sig is not None:
                        ins.then_inc(get_sem(op.sig[0]), 16 if op.is_dma else 1)
                if e == "sp":
                    last = {}
                    for op in self.ops[e]:
                        if op.is_dma:
                            last[op.sig[0]] = max(last.get(op.sig[0], 0), op.sig[1])
                    for sname, v in last.items():
                        eng.wait_ge(get_sem(sname), v)
            return body

        for e in self.ENGS:
            if self.ops[e]:
                dec[e](make(e))


def build_program(depth=DEPTH):
    nc = bass.Bass("TRN2", target_bir_lowering=False)
    S = Sched()

    def din(name, shape):
        return nc.dram_tensor(name, list(shape), F32, kind="ExternalInput").ap()

    def dout(name, shape):
        return nc.dram_tensor(name, list(shape), F32, kind="ExternalOutput").ap()

    x_d = din("x", (NT, D))
    cond_d = din("condT", (128, 8))
    ck_d = {"diff": din("ck_diff", (2, L_CTX, D)), "na": din("ck_na", (2, L_CTX, D))}
    cv_d = {"diff": din("cv_diff", (2, L_CTX, D)), "na": din("cv_na", (2, L_CTX, D))}
    w_ada_d = din("w_ada", (DEPTH, D, 6 * D))
    b_ada_d = din("b_adaT", (DEPTH, 128, 48))
    gmix_d = din("g_mixT", (DEPTH, 128, 8))
    gmlp_d = din("g_mlpT", (DEPTH, 128, 8))
    w_fc1_d = din("w_fc1", (DEPTH, D, 4 * D))
    w_fc2_d = din("w_fc2", (DEPTH, 4 * D, D))
    w_qkv_d = {"diff": din("w_qkv_diff", (2, D, 3 * D)), "na": din("w_qkv_na", (2, D, 3 * D))}
    w_o_d = {"diff": din("w_o_diff", (2, D, D)), "na": din("w_o_na", (2, D, D))}
    qkg_d = din("qkg", (128, 8))
    subg_d = din("subg", (128, 2))
    lamv_d = din("lamv", (64, 8))
    ident_d = din("ident", (128, 128))
    ones_d = din("ones", (128, 128))
    bones_d = din("blockones", (128, 128))
    rrope_d = din("rropeT", (128, 128))
    ecomb_d = din("ecomb", (62, 191))
    colmask_d = din("colmask", (128, 64))
    kind_d = din("kind", (64, NKEY))
    qind_d = {"diff": din("qind_diff", (64, NT)), "na": din("qind_na", (64, NT))}
    cos_d = din("cosT", (128, NT))
    sin_d = din("sinT", (128, NT))
    relt2_d = din("relT2", (2, 62, 240))

    y_d = dout("y", (NT, D))
    nk_d = {"diff": dout("nk_diff", (2, NT, D)), "na": dout("nk_na", (2, NT, D))}
    nv_d = {"diff": dout("nv_diff", (2, NT, D)), "na": dout("nv_na", (2, NT, D))}

    es = ExitStack()
    with es:
        def sb(name, shape, dt):
            return es.enter_context(nc.sbuf_tensor(name, list(shape), dt))

        xT = sb("xT", (128, 8, NT), F32)
        hT = sb("hT", (128, 8, NT), BF16)
        OT = sb("OT", (128, 8, NT), BF16)
        slabs = [sb("slab%d" % i, (128, 4096), BF16) for i in range(3)]
        PTP = [sb("PTP%d" % i, (128, 1024), BF16) for i in range(2)]
        TW = [sb("TW%d" % i, (128, 1024), F32) for i in range(3)]
        TH = [sb("TH%d" % i, (128, 512), F32) for i in range(8)]
        TH2 = [TW[0][:, 0:512], TW[0][:, 512:1024], TW[1][:, 0:512], TW[1][:, 512:1024]]
        TW2b = TW[2].bitcast(BF16)
        TB2 = [TW2b[:, 0:512], TW2b[:, 512:1024]]
        modT = [sb("modT%d" % i, (128, 48), F32) for i in range(2)]
        modv = [sb("modv%d" % i, (128, 16), F32) for i in range(2)]
        badaT = sb("badaT", (128, 48), F32)
        gmixT = sb("gmixT", (128, 8), F32)
        gmlpT = sb("gmlpT", (128, 8), F32)
        condT = sb("condT_sb", (128, 8), F32)
        scb = sb("scb", (128, 8), BF16)
        qkg = sb("qkg_sb", (128, 8), F32)
        subg = sb("subg_sb", (128, 2), F32)
        subg2 = sb("subg2_sb", (128, 2), F32)
        lamv = sb("lamv_sb", (64, 8), F32)
        lamp = sb("lamp_sb", (64, 2), F32)
        lame = sb("lame_sb", (128, 2), F32)
        neglam = sb("neglam_sb", (128, 1), F32)
        ident = sb("ident_sb", (128, 128), F32)
        identb = sb("identb_sb", (128, 128), BF16)
        onesf = sb("ones_sb", (128, 128), F32)
        onesb = sb("onesb_sb", (128, 128), BF16)
        bonesb = sb("bonesb_sb", (128, 128), BF16)
        rropeb = sb("rropeb_sb", (128, 128), BF16)
        TB = [sb("TB%d" % i, (128, 512), BF16) for i in range(3)]
        ecombb = sb("ecombb_sb", (62, 191), BF16)
        relhi = sb("relhi_sb", (62, 120), BF16)
        rello = sb("rello_sb", (62, 120), BF16)
        colmask = sb("colmask_sb", (128, 64), F32)
        relt2 = sb("relt2_sb", (62, 120), F32)
        arena = sb("arena", (128, 40960), BF16)
        arena32 = arena.bitcast(F32)
        off = 0
        qt = [[None, None], [None, None]]
        kt = [[None, None], [None, None]]
        for b in range(2):
            for s in range(2):
                qt[b][s] = arena[:, off:off + NT]
                off += NT
                kt[b][s] = arena[:, off:off + NKEY]
                off += NKEY
        Vb = arena[:, off:off + 10 * D].rearrange("p (t e) -> p t e", t=10)
        off += 10 * D
        ctok = arena[:, off:off + 2 * D].rearrange("p (t e) -> p t e", t=2)
        off += 2 * D
        assert off % 2 == 0
        off32 = off // 2
        GT = arena32[:, off32:off32 + 8 * 15 * 64]
        cosT = arena32[:, off32:off32 + NT]
        sinT = arena32[:, off32 + NT:off32 + 2 * NT]
        off32 += 8 * 15 * 64
        kst = [arena32[:, off32 + i * 512:off32 + (i + 1) * 512] for i in range(2)]
        off32 += 1024
        vst = [arena32[:, off32 + i * 512:off32 + (i + 1) * 512] for i in range(2)]
        off32 += 1024
        assert off32 * 2 <= 40960, off32
        h1T = arena[:, 0:32 * NT].rearrange("p (c t) -> p c t", c=32)
        ARENA_KEYS = []

        SS = es.enter_context(nc.psum_tensor("bankS", [128, 1024], F32))
        banks = [SS[:, 0:512], SS[:, 512:1024]] + \
            [es.enter_context(nc.psum_tensor("bank%d" % i, [128, 512], F32)) for i in range(2, 8)]

        def PS(b):
            return ("ps", b)

        def dma_sp(out, in_, reads=(), writes=()):
            return S.add("sp", lambda e, out=out, in_=in_: e.dma_start(out=out, in_=in_), reads, writes, dma=True)

        def dma_cast(out, in_, reads=(), writes=()):
            return S.add("pool", lambda e, out=out, in_=in_: e.dma_start(out=out, in_=in_), reads, writes, dma=True)

        def mm(out, lhsT, rhs, start, stop, reads, writes):
            return S.add("pe", lambda e, out=out, lhsT=lhsT, rhs=rhs, start=start, stop=stop:
                         e.matmul(out, lhsT, rhs, start=start, stop=stop), reads, writes)

        def tr(out, in_, idt, reads, writes):
            return S.add("pe", lambda e, out=out, in_=in_, idt=idt: e.transpose(out, in_, idt), reads, writes)

        def act(out, in_, func, reads, writes, bias=None, scale=None):
            kw = {}
            if bias is not None:
                kw["bias"] = bias
            if scale is not None:
                kw["scale"] = scale
            return S.add("act", lambda e, out=out, in_=in_, func=func, kw=kw:
                         e.activation(out=out, in_=in_, func=func, **kw), reads, writes)

        def stt(out, in0, scalar, in1, op0, op1, reads, writes):
            return S.add("dve", lambda e, out=out, in0=in0, scalar=scalar, in1=in1, op0=op0, op1=op1:
                         e.scalar_tensor_tensor(out=out, in0=in0, scalar=scalar, in1=in1, op0=op0, op1=op1),
                         reads, writes)

        def tt(out, in0, in1, op, reads, writes):
            return S.add("dve", lambda e, out=out, in0=in0, in1=in1, op=op:
                         e.tensor_tensor(out=out, in0=in0, in1=in1, op=op), reads, writes)

        def ts(out, in0, s1, s2, op0, op1, reads, writes):
            if s2 is None:
                return S.add("dve", lambda e, out=out, in0=in0, s1=s1, op0=op0:
                             e.tensor_scalar(out, in0, s1, None, op0=op0), reads, writes)
            return S.add("dve", lambda e, out=out, in0=in0, s1=s1, s2=s2, op0=op0, op1=op1:
                         e.tensor_scalar(out, in0, s1, s2, op0=op0, op1=op1), reads, writes)

        def vcopy(out, in_, reads, writes):
            return S.add("dve", lambda e, out=out, in_=in_: e.tensor_copy(out=out, in_=in_), reads, writes)

        def ptt(out, in0, in1, op, reads, writes):
            return S.add(POOL_ENG, lambda e, out=out, in0=in0, in1=in1, op=op:
                         e.tensor_tensor(out=out, in0=in0, in1=in1, op=op), reads, writes)

        def pcopy(out, in_, reads, writes):
            return S.add(POOL_ENG, lambda e, out=out, in_=in_: e.tensor_copy(out=out, in_=in_), reads, writes)

        def recip(out, in_, reads, writes):
            return S.add("dve", lambda e, out=out, in_=in_: e.reciprocal(out=out, in_=in_), reads, writes)

        slab_ctr = [0]

        def next_slab():
            i = slab_ctr[0] % 3
            slab_ctr[0] += 1
            return i

        def load_slab_k8(w2d, col0, width, col_off=0, si=None):
            if si is None:
                si = next_slab()
            return si

        def load_consts_and_x():
            C = "const"
            dma_sp(ident[:], ident_d, (), [("c", "ident")])
            dma_cast(identb[:], ident_d, (), [("c", "identb")])
            dma_sp(onesf[:], ones_d, (), [("c", "ones")])
            dma_cast(onesb[:], ones_d, (), [("c", "onesb")])
            dma_cast(bonesb[:], bones_d, (), [("c", "bones")])
            dma_cast(rropeb[:], rrope_d, (), [("c", "rrope")])
            dma_cast(ecombb[:], ecomb_d, (), [("c", "ecomb")])
            dma_sp(colmask[:], colmask_d, (), [("c", "colmask")])
            dma_sp(qkg[:], qkg_d, (), [("c", "qkg")])
            dma_sp(subg[:], subg_d, (), [("c", "subg")])
            dma_sp(lamv[:], lamv_d, (), [("c", "lamv")])
            for b in range(2):
                dma_cast(kt[b][0][64:128, :], kind_d, (), [("kt", b, 0, "ind")])
                dma_cast(kt[b][1][0:64, :], kind_d, (), [("kt", b, 1, "ind")])

            for t in range(8):
                tw = TW[t % 2]
                twk = ("TW", t % 2)
                dma_sp(tw[:], x_d[t * 128:(t + 1) * 128, :], (), [twk])
                for g in range(2):
                    bk = (2 * t + g) % 4
                    for cc in range(4):
                        c = g * 4 + cc
                        tr(banks[bk][:, cc * 128:(cc + 1) * 128], tw[:, c * 128:(c + 1) * 128], ident[:],
                           [twk, ("c", "ident")], [PS(bk)])
                    S.add("act", lambda e, bk=bk, g=g, t=t: e.activation(
                        out=xT[:, g * 4:(g + 1) * 4, t * 128:(t + 1) * 128],
                        in_=banks[bk][:, :].rearrange("p (c n) -> p c n", c=4), func=AF.Copy),
                        [PS(bk)], [("xT", c2) for c2 in range(g * 4, g * 4 + 4)])


        def ada_steps(l):
            steps = []
            par = l % 2
            for s in range(12):
                def step(s=s):
                    if s == 0:
                        dma_sp(badaT[:], b_ada_d[l], (), [("c", "bada")])
                    si = next_slab()
                    sl = slabs[si]
                    dma_cast(sl[:, 0:4096].rearrange("p (k n) -> p k n", k=8),
                             w_ada_d[l][:, s * 512:(s + 1) * 512].rearrange("(k p) n -> p k n", p=128),
                             (), [("slab", si)])
                    for m in range(4):
                        col = s * 4 + m
                        for kc in range(8):
                            mm(banks[6][:, col:col + 1], sl[:, kc * 512 + m * 128: kc * 512 + (m + 1) * 128],
                               scb[:, kc:kc + 1], kc == 0, kc == 7,
                               [("slab", si), ("c", "scb")], [PS(6)])
                    tt(modT[par][:, s * 4:(s + 1) * 4], banks[6][:, s * 4:(s + 1) * 4], badaT[:, s * 4:(s + 1) * 4],
                       ALU.add, [PS(6), ("c", "bada")], [("mod", par, s)])
                steps.append(step)
            return steps

        def mod_derive(l, which):
            par = l % 2
            if which == 0:
                dma_sp(gmixT[:], gmix_d[l], (), [("c", "gmix")])
                stt(modv[par][:, 0:8], modT[par][:, 8:16], 1.0, gmixT[:], ALU.add, ALU.mult,
                    [("mod", par, 2), ("mod", par, 3), ("c", "gmix")], [("modv", par, 0)])
            else:
                dma_sp(gmlpT[:], gmlp_d[l], (), [("c", "gmlp")])
                stt(modv[par][:, 8:16], modT[par][:, 32:40], 1.0, gmlpT[:], ALU.add, ALU.mult,
                    [("mod", par, 8), ("mod", par, 9), ("c", "gmlp")], [("modv", par, 1)])

        ada_queue = []

        def pump_ada(n=1):
            for _ in range(n):
                if ada_queue:
                    ada_queue.pop(0)()

        def norm_sq(c):
            for hf in range(2):
                act(TB[hf][:], xT[:, c, hf * 512:(hf + 1) * 512], AF.Square, [("xT", c)], [("TB", hf)])

        def norm_ss(c):
            for hf in range(2):
                mm(banks[4 + hf][:], onesb[:], TB[hf][:], c == 0, c == 7,
                   [("TB", hf), ("c", "onesb")], [PS(4 + hf)])

        def norm_stats(c):
            norm_sq(c)
            norm_ss(c)

        def norm_stats_delayed(m):
            if m > 0:
                norm_ss(m - 1)
            norm_sq(m)
            if m == 7:
                norm_ss(7)

        def norm_mod(l, which, have_stats=False):
            par = l % 2
            gs = modv[par][:, 0:8] if which == 0 else modv[par][:, 8:16]
            shift = modT[par][:, 0:8] if which == 0 else modT[par][:, 24:32]
            mk = [("mod", par, 0 + 6 * which), ("mod", par, 1 + 6 * which), ("modv", par, which)]
            if not have_stats:
                for c in range(8):
                    norm_stats(c)
            rs = TW[2]
            for hf in range(2):
                act(rs[:, hf * 512:(hf + 1) * 512], banks[4 + hf][:], AF.Ln, [PS(4 + hf)], [("TW", 2, hf)],
                    bias=EPS, scale=1.0 / D)
                act(rs[:, hf * 512:(hf + 1) * 512], rs[:, hf * 512:(hf + 1) * 512], AF.Exp, [("TW", 2, hf)],
                    [("TW", 2, hf)], scale=-0.5)
            for c in range(8):
                tw = TW[c % 2]
                twk = ("TW", c % 2)
                tt(tw[:], xT[:, c, :], rs[:], ALU.mult, [("xT", c), ("TW", 2, 0), ("TW", 2, 1)], [twk])
                act(hT[:, c, :], tw[:], AF.Identity, [twk] + mk, [("hT", c)], bias=shift[:, c:c + 1],
                    scale=gs[:, c:c + 1])

        def attention_layer(l):
            T = "diff" if l % 2 == 0 else "na"
            i = l // 2
            par = l % 2
            wq = w_qkv_d[T][i]
            lam_init = 0.8 - 0.6 * math.exp(-0.3 * l)
            gate = modT[par][:, 16:24]
            qg = qkg[:, (0 if T == "diff" else 4) + i:(0 if T == "diff" else 4) + i + 1]
            kg = qkg[:, (2 if T == "diff" else 6) + i:(2 if T == "diff" else 6) + i + 1]

            norm_mod(l, 0, have_stats=(l > 0))
            _stage('norm')

            for b in range(2):
                dma_cast(qt[b][0][64:128, :], qind_d[T], (), [("qt", b, 0, "ind")])
                dma_cast(qt[b][1][0:64, :], qind_d[T], (), [("qt", b, 1, "ind")])
            dma_cast(ctok[:, :, :], ck_d[T][i].rearrange("(t p) e -> p t e", p=128), (), [("ctok",)])
            dma_cast(Vb[:, 8:10, :], cv_d[T][i].rearrange("(t p) e -> p t e", p=128), (), [("V", 8), ("V", 9)])

            if T == "diff":
                dma_sp(cosT, cos_d, (), [("GT",)])
                dma_sp(sinT, sin_d, (), [("GT",)])
                tt(lamp[:, 0:1], lamv[:, i * 4:i * 4 + 1], lamv[:, i * 4 + 1:i * 4 + 2], ALU.mult,
                   [("c", "lamv")], [("lamp",)])
                tt(lamp[:, 1:2], lamv[:, i * 4 + 2:i * 4 + 3], lamv[:, i * 4 + 3:i * 4 + 4], ALU.mult,
                   [("c", "lamv"), ("lamp",)], [("lamp",)])
                mm(banks[7][:, 0:2], onesf[0:64, :], lamp[:, :], True, True, [("lamp",), ("c", "ones")], [PS(7)])
                act(lame[:], banks[7][:, 0:2], AF.Exp, [PS(7)], [("lame",)])
                stt(neglam[:], lame[:, 1:2], -lam_init, lame[:, 0:1], ALU.add, ALU.subtract,
                    [("lame",)], [("neglam",)])
                ts(subg2[:, i:i + 1], subg[:, i:i + 1], 1.0 - lam_init, None, ALU.mult, None,
                   [("c", "subg")], [("subg2", i)])

            _stage('pre')
            def v_projection():
                vctr = 0
                for s in range(2):
                    si = next_slab()
                    sl = slabs[si]
                    dma_cast(sl[:, 0:4096].rearrange("p (k n) -> p k n", k=8),
                             wq[:, 2048 + s * 512:2048 + (s + 1) * 512].rearrange("(k p) n -> p k n", p=128),
                             (), [("slab", si)])
                    for t in range(8):
                        bk = vctr % 4
                        for kc in range(8):
                            mm(banks[bk][:], hT[:, kc, t * 128:(t + 1) * 128], sl[:, kc * 512:(kc + 1) * 512],
                               kc == 0, kc == 7, [("slab", si), ("hT", kc)], [PS(bk)])
                        st = vst[vctr % 2]
                        stk = ("vst", vctr % 2)
                        act(st, banks[bk][:], AF.Copy, [PS(bk)], [stk])
                        vcopy(Vb[:, t, s * 512:(s + 1) * 512], banks[bk][:], [PS(bk)], [("V", t)])
                        dma_sp(nv_d[T][i][t * 128:(t + 1) * 128, s * 512:(s + 1) * 512], st, [stk], ())
                        vctr += 1
                        for _ in range(2):
                            step(bg_lo)
                            step(bg_k)

            _stage('vproj')
            def build_gtable(hg):
                dma_sp(relt2[:], relt2_d[i][:, hg * 120:(hg + 1) * 120], (), [("c", "relt2")])
                vcopy(relhi[:], relt2[:], [("c", "relt2")], [("c", "relhi")])
                tt(rello[:], relt2[:], relhi[:], ALU.subtract, [("c", "relt2"), ("c", "relhi")], [("c", "rello")])
                for qc in range(64):
                    bk = (6, 7, 0, 1)[qc % 4]
                    mm(banks[bk][:, 0:120], ecombb[:, 63 - qc:191 - qc], relhi[:, :],
                       True, False, [("c", "ecomb"), ("c", "relhi")], [PS(bk)])
                    mm(banks[bk][:, 0:120], ecombb[:, 63 - qc:191 - qc], rello[:, :],
                       False, True, [("c", "ecomb"), ("c", "rello")], [PS(bk)])
                    S.add("dve", lambda e, bk=bk, qc=qc: e.tensor_scalar(
                        GT.rearrange("p (a q) -> p a q", q=64)[:, :, qc], banks[bk][:, 0:120],
                        8.0, colmask[:, qc:qc + 1], op0=ALU.mult, op1=ALU.add),
                        [PS(bk), ("c", "colmask")], [("GT",)])

            def prep_load(c):
                si = next_slab()
                sl = slabs[si]
                dma_cast(sl[:, 0:1024].rearrange("p (k n) -> p k n", k=8),
                         wq[:, c * 128:(c + 1) * 128].rearrange("(k p) n -> p k n", p=128),
                         (), [("slab", si)])
                dma_cast(sl[:, 1024:2048].rearrange("p (k n) -> p k n", k=8),
                         wq[:, 1024 + c * 128:1024 + (c + 1) * 128].rearrange("(k p) n -> p k n", p=128),
                         (), [("slab", si)])
                return si

            def prep(c, X, si):
                b = c % 2
                sl = slabs[si]
                bk = 6 + X
                if X == 0:
                    tq, tsq, trs, tn = TH[0], TH[1], TH[2], TH[3]
                    tb0, tb1 = TB[0], TB[1]
                    kq, ksq, krs, kn, kb0, kb1 = ("TH", 0), ("TH", 1), ("TH", 2), ("TH", 3), ("TB", 0), ("TB", 1)
                else:
                    tq, tsq, trs, tn = TH2
                    tb0, tb1 = TB2[0], TB2[1]
                    kq, ksq, krs, kn, kb0, kb1 = ("TH2", 0), ("TH2", 1), ("TH2", 2), ("TH2", 3), ("TB2", 0), ("TB2", 1)
                    for j in range(2):
                        mm(banks[bk][:, j * 128:(j + 1) * 128], ctok[:, j, c * 128:(c + 1) * 128], identb[:],
                           True, True, [("ctok",), ("c", "identb")], [PS(bk)])
                    yield
                    vcopy(kt[b][0][0:64, NT:NKEY], banks[bk][0:64, 0:256], [PS(bk)], [("kt", b, 0, "ctx")])
                    vcopy(kt[b][1][64:128, NT:NKEY], banks[bk][64:128, 0:256], [PS(bk)],
                          [("kt", b, 1, "ctx")])
                    yield
                g = qg if X == 0 else kg
                dst = qt if X == 0 else kt
                nm = "qt" if X == 0 else "kt"
                for hf in range(2):
                    cs = slice(hf * 512, (hf + 1) * 512)
                    for kc in range(8):
                        mm(banks[bk][:], sl[:, X * 1024 + kc * 128:X * 1024 + (kc + 1) * 128], hT[:, kc, cs],
                           kc == 0, kc == 7, [("slab", si), ("hT", kc)], [PS(bk)])
                        if kc % 2 == 1:
                            yield
                    vcopy(tq[:], banks[bk][:], [PS(bk)], [kq])
                    yield
                    tt(tb0[:], tq[:], tq[:], ALU.mult, [kq], [kb0])
                    yield
                    mm(banks[bk][:], bonesb[:], tb0[:], True, True, [kb0, ("c", "bones")], [PS(bk)])
                    yield
                    act(trs[:], banks[bk][:], AF.Ln, [PS(bk)], [krs], bias=EPS, scale=1.0 / 64)
                    yield
                    act(trs[:], trs[:], AF.Exp, [krs], [krs], scale=-0.5)
                    yield
                    stt(tn[:], tq[:], g, trs[:], ALU.mult, ALU.mult, [kq, krs, ("c", "qkg")], [kn])
                    fin = tn
                    fink = kn
                    yield
                    if T == "diff":
                        vcopy(tb1[:], tn[:], [kn], [kb1])
                        tt(tq[:], tn[:], cosT[:, cs], ALU.mult, [kn, ("GT",)], [kq])
                        yield
                        mm(banks[bk][:], rropeb[:], tb1[:], True, True, [kb1, ("c", "rrope")], [PS(bk)])
                        yield
                        tt(tsq[:], banks[bk][:], sinT[:, cs], ALU.mult, [PS(bk), ("GT",)], [ksq])
                        yield
                        tt(trs[:], tq[:], tsq[:], ALU.add, [kq, ksq], [krs])
                        fin = trs
                        fink = krs
                        yield
                    vcopy(dst[b][0][0:64, cs], fin[0:64, :], [fink], [(nm, b, 0, "lat", hf)])
                    vcopy(dst[b][1][64:128, cs], fin[64:128, :], [fink], [(nm, b, 1, "lat", hf)])
                    if X == 1:
                        for tb in range(4):
                            tr(banks[bk][:, tb * 128:(tb + 1) * 128], fin[:, tb * 128:(tb + 1) * 128], ident[:],
                               [fink, ("c", "ident")], [PS(bk)])
                        yield
                        ks = kst[hf]
                        vcopy(ks, banks[bk][:], [PS(bk)], [("kst", hf)])
                        dma_sp(nk_d[T][i][hf * 512:(hf + 1) * 512, c * 128:(c + 1) * 128]
                               .rearrange("(t p) e -> p t e", p=128),
                               ks.rearrange("p (t e) -> p t e", t=4), [("kst", hf)], ())
                    yield

            def epilogue(c, hq, s, bO, bZ):
                qs = slice(hq * 512, (hq + 1) * 512)
                rz = TH[4]
                act(rz[:], banks[bZ][:], AF.Ln, [PS(bZ)], [("TH", 4)])
                yield
                act(rz[:], rz[:], AF.Exp, [("TH", 4)], [("TH", 4)], scale=-1.0)
                yield
                if T == "na":
                    ps_ = slice(0, 64) if s == 0 else slice(64, 128)
                    tt(OT[ps_, c, qs], banks[bO][ps_, :], rz[ps_, :], ALU.mult, [PS(bO), ("TH", 4)],
                       [("OT", c, hq, s)])
                    return
                if s == 0:
                    tt(TH[5][:], banks[bO][:], rz[:], ALU.mult, [PS(bO), ("TH", 4)], [("TH", 5)])
                    return
                tt(TH[6][:], banks[bO][:], rz[:], ALU.mult, [PS(bO), ("TH", 4)], [("TH", 6)])
                yield
                stt(TH[7][:], TH[6][:], neglam[:, 0:1], TH[5][:], ALU.mult, ALU.add,
                    [("TH", 5), ("TH", 6), ("neglam",)], [("TH", 7)])
                yield
                tt(TB[2][:], TH[7][:], TH[7][:], ALU.mult, [("TH", 7)], [("TB", 2)])
                yield
                mm(banks[bZ][:], onesb[:], TB[2][:], True, True, [("TB", 2), ("c", "onesb")], [PS(bZ)])
                yield
                act(TH[5][:], banks[bZ][:], AF.Ln, [PS(bZ)], [("TH", 5)], bias=EPS, scale=1.0 / 128)
                yield
                act(TH[5][:], TH[5][:], AF.Exp, [("TH", 5)], [("TH", 5)], scale=-0.5)
                yield
                stt(OT[:, c, qs], TH[7][:], subg2[:, i:i + 1], TH[5][:], ALU.mult, ALU.mult,
                    [("TH", 7), ("TH", 5), ("subg2", i)], [("OT", c, hq, 0), ("OT", c, hq, 1)])

            bg_hi = []
            bg_lo = []

            def step(q):
                while q:
                    try:
                        next(q[0])
                        return True
                    except StopIteration:
                        q.pop(0)
                return False

            def drain(q, keep=0):
                while len(q) > keep:
                    g0 = q[0]
                    for _ in g0:
                        pass
                    q.pop(0)

            bg_k = []

            def pump():
                step(bg_hi)
                step(bg_lo)
                step(bg_k)

            pt_ctr = [0]
            st_ctr = [0]
            acc_ctr = [0]

            streams = {}
            seq = []
            for c in range(8):
                for hq in range(2):
                    if T == "diff":
                        blocks = list(range(10))
                    else:
                        blocks = ([0, 1, 2, 3, 4, 5] if hq == 0 else [2, 3, 4, 5, 6, 7]) + [8, 9]
                    for s in range(2):
                        k = (c, hq, s)
                        streams[k] = blocks
                        seq += [(k, p) for p in range(len(blocks) // 2)]
            pis = {}
            acc = {}
            started = set()

            def finish_prep():
                while bg_lo or bg_k:
                    step(bg_lo)
                    step(bg_k)

            def ensure_chunk(c):
                if c in started:
                    return
                started.add(c)
                finish_prep()
                if T == "na" and c == 4:
                    build_gtable(1)
                if c + 1 < 8:
                    si1 = prep_load(c + 1)
                    bg_lo.append(prep(c + 1, 0, si1))
                    bg_k.append(prep(c + 1, 1, si1))

            def issue_s_pair(k, p):
                c, hq, s = k
                ensure_chunk(c)
                b = c % 2
                blocks = streams[k]
                qs = slice(hq * 512, (hq + 1) * 512)
                for h in range(2):
                    j = blocks[2 * p + h]
                    kr = [("kt", b, s, "ind"), ("qt", b, s, "ind"), ("qt", b, s, "lat", hq)]
                    kr.append(("kt", b, s, "ctx") if j >= 8 else ("kt", b, s, "lat", j // 4))
                    mm(banks[h], kt[b][s][:, j * 128:(j + 1) * 128], qt[b][s][:, qs], True, True,
                       kr, [PS(h)])
                    if T == "na" and j < 8:
                        a = 2 * j
                        lo, hi = (0, a + 5) if a <= 6 else (a - 3, 15)
                        lo = max(lo, 8 * hq)
                        hi = min(hi, 8 * hq + 7)
                        hl = (2 * c + s) % 8
                        sp0 = 7 - a + lo
                        n = (hi - lo + 1) * 64
                        go = (hl * 15 + sp0) * 64
                        po = h * 512 + (lo - 8 * hq) * 64
                        tt(SS[:, po:po + n], SS[:, po:po + n], GT[:, go:go + n], ALU.add,
                           [PS(h), ("GT",)], [PS(h)])
                pi = pt_ctr[0] % 2
                pt_ctr[0] += 1
                act(PTP[pi][:], SS[:, :], AF.Exp, [PS(0), PS(1)], [("PT", pi)], scale=0.125)
                pis[(k, p)] = pi

            def issue_av_pair(k, p):
                c, hq, s = k
                blocks = streams[k]
                nb = len(blocks)
                if p == 0:
                    drain(bg_hi, keep=1)
                    aset = acc_ctr[0] % 2
                    acc_ctr[0] += 1
                    acc[k] = (2 + 2 * aset, 3 + 2 * aset)
                bO, bZ = acc[k]
                pi = pis[(k, p)]
                for h in range(2):
                    bi = 2 * p + h
                    j = blocks[bi]
                    rhs = PTP[pi][:, h * 512:(h + 1) * 512]
                    mm(banks[bO][:], Vb[:, j, c * 128:(c + 1) * 128], rhs, bi == 0, bi == nb - 1,
                       [("PT", pi), ("V", j)], [PS(bO)])
                    mm(banks[bZ][:], onesb[:], rhs, bi == 0, bi == nb - 1,
                       [("PT", pi), ("c", "onesb")], [PS(bZ)])
                if p == nb // 2 - 1:
                    bg_hi.append(epilogue(c, hq, s, bO, bZ))

            if T == "na":
                build_gtable(0)
            arena_fence(False)
            si0 = prep_load(0)
            bg_lo.append(prep(0, 0, si0))
            bg_k.append(prep(0, 1, si0))
            v_projection()
            issue_s_pair(*seq[0])
            for g, (k, p) in enumerate(seq):
                if g + 1 < len(seq):
                    issue_s_pair(*seq[g + 1])
                issue_av_pair(k, p)
                pump()
                pump()
            finish_prep()
            drain(bg_hi)

            _stage('attn')
            if l == 0:
                pump_ada(2)
            octr = 0
            for s in range(2):
                si = next_slab()
                sl = slabs[si]
                dma_cast(sl[:, 0:4096].rearrange("p (k n) -> p k n", k=8),
                         w_o_d[T][i][:, s * 512:(s + 1) * 512].rearrange("(k p) n -> p k n", p=128),
                         (), [("slab", si)])
                for m4 in range(4):
                    m = s * 4 + m4
                    for hf in range(2):
                        bk = octr % 4
                        octr += 1
                        cs = slice(hf * 512, (hf + 1) * 512)
                        for kc in range(8):
                            mm(banks[bk][:], sl[:, kc * 512 + m4 * 128:kc * 512 + (m4 + 1) * 128], OT[:, kc, cs],
                               kc == 0, kc == 7,
                               [("slab", si), ("OT", kc, hf, 0), ("OT", kc, hf, 1)], [PS(bk)])
                        stt(xT[:, m, cs], banks[bk][:], gate[:, m:m + 1], xT[:, m, cs], ALU.mult, ALU.add,
                            [PS(bk), ("xT", m), ("mod", par, 4), ("mod", par, 5)], [("xT", m)])
                    norm_stats_delayed(m)

        def mlp_layer(l):
            par = l % 2
            gate = modT[par][:, 40:48]
            _stage('oproj')
            if l == 0:
                pump_ada(4)
            mod_derive(l, 1)
            norm_mod(l, 1, have_stats=True)
            _stage('norm2')
            ctr = 0
            for s in range(8):
                si = next_slab()
                sl = slabs[si]
                dma_cast(sl[:, 0:4096].rearrange("p (k n) -> p k n", k=8),
                         w_fc1_d[l][:, s * 512:(s + 1) * 512].rearrange("(k p) n -> p k n", p=128),
                         (), [("slab", si)])
                for m4 in range(4):
                    f = s * 4 + m4
                    for hf in range(2):
                        bk = ctr % 4
                        cs = slice(hf * 512, (hf + 1) * 512)
                        for kc in range(8):
                            mm(banks[bk][:], sl[:, kc * 512 + m4 * 128:kc * 512 + (m4 + 1) * 128], hT[:, kc, cs],
                               kc == 0, kc == 7, [("slab", si), ("hT", kc)], [PS(bk)])
                        th = TH[ctr % 4]
                        thk = ("TH", ctr % 4)
                        act(th[:], banks[bk][:], AF.Relu, [PS(bk)], [thk])
                        tt(h1T[:, f, cs], th[:], th[:], ALU.mult, [thk], [("h1T", f, hf)])
                        ctr += 1
                pump_ada(1)
            _stage('fc1')
            for m in range(8):
                si = next_slab()
                sl = slabs[si]
                dma_cast(sl[:, 0:4096].rearrange("p (k n) -> p k n", k=32),
                         w_fc2_d[l][:, m * 128:(m + 1) * 128].rearrange("(k p) n -> p k n", p=128),
                         (), [("slab", si)])
                for hf in range(2):
                    bk = ctr % 4
                    ctr += 1
                    cs = slice(hf * 512, (hf + 1) * 512)
                    for kc in range(32):
                        mm(banks[bk][:], sl[:, kc * 128:(kc + 1) * 128], h1T[:, kc, cs], kc == 0, kc == 31,
                           [("slab", si), ("h1T", kc, hf)], [PS(bk)])
                    stt(xT[:, m, cs], banks[bk][:], gate[:, m:m + 1], xT[:, m, cs], ALU.mult, ALU.add,
                        [PS(bk), ("xT", m), ("mod", par, 10), ("mod", par, 11)], [("xT", m)])
                if l + 1 < depth:
                    norm_stats_delayed(m)
                pump_ada(1)

        arena_att_keys = []
        for b in range(2):
            for s in range(2):
                arena_att_keys += [("qt", b, s, "ind"), ("qt", b, s, "lat", 0), ("qt", b, s, "lat", 1),
                                   ("kt", b, s, "ind"), ("kt", b, s, "ctx"), ("kt", b, s, "lat", 0),
                                   ("kt", b, s, "lat", 1)]
        arena_att_keys += [("V", t) for t in range(10)] + [("ctok",), ("GT",)]
        arena_att_keys += [("kst", 0), ("kst", 1), ("vst", 0), ("vst", 1)]
        arena_mlp_keys = [("h1T", f, hf) for f in range(32) for hf in range(2)]

        def arena_fence(to_mlp):
            keys = arena_att_keys + arena_mlp_keys + [("TW", 0), ("TW", 1), ("TW", 2, 0), ("TW", 2, 1)] + \
                [("TH2", j) for j in range(4)] + [("TB2", 0), ("TB2", 1)]
            S.add("dve", lambda e: e.tensor_copy(out=lamp[:, 0:1], in_=lamp[:, 0:1]), keys, keys + [("lamp",)])

        dma_sp(condT[:], cond_d, (), [("c", "cond")])
        act(scb[:], condT[:], AF.Silu, [("c", "cond")], [("c", "scb")])
        ada_queue.extend(ada_steps(0))
        pump_ada(4)
        load_consts_and_x()
        try:
            for l in range(depth):
                mod_derive(l, 0)
                if l > 0:
                    arena_fence(False)
                    for b in range(2):
                        dma_cast(kt[b][0][64:128, :], kind_d, (), [("kt", b, 0, "ind")])
                        dma_cast(kt[b][1][0:64, :], kind_d, (), [("kt", b, 1, "ind")])
                attention_layer(l)
                arena_fence(True)
                if l + 1 < depth:
                    ada_queue.extend(ada_steps(l + 1))
                mlp_layer(l)
                pump_ada(len(ada_queue))
        except _Stop:
            pass

        for t in range(8):
            tw = TW[t % 2]
            twk = ("TW", t % 2)
            for g in range(2):
                bk = (2 * t + g) % 4
                for cc in range(4):
                    c = g * 4 + cc
                    tr(banks[bk][:, cc * 128:(cc + 1) * 128], xT[:, c, t * 128:(t + 1) * 128], ident[:],
                       [("xT", c), ("c", "ident")], [PS(bk)])
                act(tw[:, g * 512:(g + 1) * 512], banks[bk][:], AF.Copy, [PS(bk)], [twk])
            dma_sp(y_d[t * 128:(t + 1) * 128, :], tw[:], [twk], ())

        S.lower(nc, es)
    return nc


def _const_tables():
    ident = np.eye(128, dtype=np.float32)
    ones = np.ones((128, 128), np.float32)
    bones = np.zeros((128, 128), np.float32)
    bones[:64, :64] = 1.0
    bones[64:, 64:] = 1.0
    rr = np.zeros((128, 128), np.float32)
    for m in range(128):
        d = m % 64
        if (d % 32) < 16:
            rr[m + 16, m] = -1.0
        else:
            rr[m - 16, m] = 1.0
    ecomb = np.zeros((62, 191), np.float32)
    for j in range(31):
        ecomb[j, j + 48] = 1.0
        ecomb[31 + j, j + 112] = 1.0
    return ident, ones, bones, rr, ecomb


def _core_tables(is_sample):
    rows = np.arange(NT) // 64
    cols = np.arange(NT) % 64
    kind = np.zeros((64, NKEY), np.float32)
    kind[rows, np.arange(NT)] = 1.0
    kind[16, NT:] = 1.0
    kind[17, :] = 1.0

    def qind(win, ctxvis):
        q = np.zeros((64, NT), np.float32)
        q[0:16, :] = BIGM * win[:, rows]
        q[16, :] = BIGM * ctxvis
        q[17, :] = -BIGM
        return q

    r = np.arange(16)
    if is_sample:
        win_diff = np.ones((16, 16), np.float32)
        rs = np.clip(r - 4, 0, 8)
        win_na = ((r[:, None] >= rs[None, :]) & (r[:, None] < rs[None, :] + 8)).astype(np.float32)
        ctxvis = 1.0
        kc = np.arange(64)[:, None]
        qc = np.arange(64)[None, :]
        cs = np.clip(qc - 8, 0, 48)
        valid = (kc >= cs) & (kc < cs + 16)
        colmask = np.where(valid, 0.0, -BIGM).astype(np.float32)
        colmask = np.concatenate([colmask, colmask], axis=0)
        n_freq = 16
        freqs = (np.float32(10000.0) ** (-np.arange(n_freq, dtype=np.float32) / np.float32(n_freq))).astype(np.float32)
        ang_row = rows.astype(np.float32)[None, :] * freqs[:, None]
        ang_col = cols.astype(np.float32)[None, :] * freqs[:, None]
        ang = np.zeros((128, NT), np.float32)
        for p in range(128):
            d = p % 64
            ang[p] = ang_row[d % 16] if d < 32 else ang_col[d % 16]
        cosT = np.cos(ang).astype(np.float32)
        sinT = np.sin(ang).astype(np.float32)
    else:
        same = (r[:, None] // 4 == r[None, :] // 4).astype(np.float32)
        win_diff = same
        win_na = same
        ctxvis = 0.0
        colmask = np.zeros((128, 64), np.float32)
        cosT = np.ones((128, NT), np.float32)
        sinT = np.zeros((128, NT), np.float32)
    return kind, qind(win_diff, ctxvis), qind(win_na, ctxvis), colmask, cosT, sinT


_NC_CACHE = {}
_RETURN_IN_MAPS = False
_DEPTH_RUN = DEPTH


def kernel(x_prompt, x_sample, cache_diff_k, cache_diff_v, cache_na_k, cache_na_v, c, c_ctx,
           w_ada, b_ada, norm_mix_g, norm_mlp_g, w_fc1, w_fc2,
           w_qkv_diff, w_o_diff, q_norm_diff_g, k_norm_diff_g,
           lambda_q1, lambda_k1, lambda_q2, lambda_k2, subln_g,
           w_qkv_na, w_o_na, q_norm_na_g, k_norm_na_g, rel_bias_na):
    f = lambda a: np.ascontiguousarray(np.asarray(a, dtype=np.float32))
    x_prompt, x_sample = f(x_prompt), f(x_sample)
    cache_diff_k, cache_diff_v, cache_na_k, cache_na_v = map(f, (cache_diff_k, cache_diff_v, cache_na_k, cache_na_v))
    c, c_ctx = f(c), f(c_ctx)
    ident, ones, bones, rr, ecomb = _const_tables()

    def colT(v, ncol):
        return np.ascontiguousarray(np.asarray(v, np.float32).reshape(ncol, 128).T)

    b_adaT = np.stack([colT(b_ada[l], 48) for l in range(DEPTH)])
    g_mixT = np.stack([colT(norm_mix_g[l], 8) for l in range(DEPTH)])
    g_mlpT = np.stack([colT(norm_mlp_g[l], 8) for l in range(DEPTH)])
    rep2 = lambda v: np.concatenate([np.asarray(v, np.float32)] * 2)
    qkg = np.stack([rep2(q_norm_diff_g[0]), rep2(q_norm_diff_g[1]), rep2(k_norm_diff_g[0]), rep2(k_norm_diff_g[1]),
                    rep2(q_norm_na_g[0]), rep2(q_norm_na_g[1]), rep2(k_norm_na_g[0]), rep2(k_norm_na_g[1])], axis=1)
    subg = np.stack([np.asarray(subln_g[0], np.float32), np.asarray(subln_g[1], np.float32)], axis=1)
    lamv = np.stack([np.asarray(v[i], np.float32) for i in range(2)
                     for v in (lambda_q1, lambda_k1, lambda_q2, lambda_k2)], axis=1)
    rel = np.asarray(rel_bias_na, np.float32)
    relT2 = np.zeros((2, 62, 240), np.float32)
    for half in range(2):
        for sp in range(15):
            dr = 14 - sp + half
            if dr > 14:
                continue
            relT2[:, half * 31:(half + 1) * 31, sp::15] = np.transpose(rel[:, :, dr, :], (0, 2, 1))
    shared = {
        "w_ada": f(w_ada), "b_adaT": f(b_adaT), "g_mixT": f(g_mixT), "g_mlpT": f(g_mlpT),
        "w_fc1": f(w_fc1), "w_fc2": f(w_fc2), "w_qkv_diff": f(w_qkv_diff), "w_qkv_na": f(w_qkv_na),
        "w_o_diff": f(w_o_diff), "w_o_na": f(w_o_na), "qkg": f(qkg), "subg": f(subg), "lamv": f(lamv),
        "ident": ident, "ones": ones, "blockones": bones, "rropeT": rr, "ecomb": ecomb, "relT2": relT2,
    }
    tabs = {True: _core_tables(True), False: _core_tables(False)}
    zc = np.zeros((2, L_CTX, D), np.float32)
    in_maps = []
    for core in range(N_CORES):
        is_sample = core < 4
        kind, qd, qn, colmask, cosT, sinT = tabs[is_sample]
        m = dict(shared)
        m.update({"kind": kind, "qind_diff": qd, "qind_na": qn, "colmask": colmask, "cosT": cosT, "sinT": sinT})
        if is_sample:
            bi = core
            m["x"] = x_sample[bi]
            m["condT"] = colT(c[bi], 8)
            m["ck_diff"] = f(cache_diff_k[bi].reshape(2, L_CTX, D))
            m["cv_diff"] = f(cache_diff_v[bi].reshape(2, L_CTX, D))
            m["ck_na"] = f(cache_na_k[bi].reshape(2, L_CTX, D))
            m["cv_na"] = f(cache_na_v[bi].reshape(2, L_CTX, D))
        else:
            p0 = (core - 4) * 4
            m["x"] = f(x_prompt[p0:p0 + 4].reshape(NT, D))
            m["condT"] = colT(c_ctx, 8)
            m["ck_diff"] = zc
            m["cv_diff"] = zc
            m["ck_na"] = zc
            m["cv_na"] = zc
        in_maps.append(m)

    if _RETURN_IN_MAPS:
        return in_maps
    if "nc" not in _NC_CACHE:
        _NC_CACHE["nc"] = build_program(_DEPTH_RUN)
    nc = _NC_CACHE["nc"]
    res = run_bass_kernel_spmd(nc, in_maps, core_ids=list(range(N_CORES)))
    R = res.results
    ys = np.stack([R[b]["y"] for b in range(4)]).astype(np.float32)
    yp = np.concatenate([R[4 + g]["y"].reshape(4, 256, D) for g in range(4)], axis=0).astype(np.float32)

    def gather_new(name, H, E):
        out = np.concatenate([np.transpose(R[4 + g][name].reshape(2, 4, 256, D), (1, 0, 2, 3)) for g in range(4)],
                             axis=0)
        return np.ascontiguousarray(out.reshape(16, 2, 256, H, E)).astype(np.float32)

    return (yp, ys, gather_new("nk_diff", 8, 128), gather_new("nv_diff", 8, 128),
            gather_new("nk_na", 16, 64), gather_new("nv_na", 16, 64))
```

```python
import math
from contextlib import ExitStack

import numpy as np
import concourse.bass as bass
import concourse.mybir as mybir
from concourse.bass_utils import run_bass_kernel_spmd

F32 = mybir.dt.float32
BF16 = mybir.dt.bfloat16
AF = mybir.ActivationFunctionType
ALU = mybir.AluOpType

D = 1024
NT = 1024
DEPTH = 4
L_CTX = 256
NKEY = NT + L_CTX
EPS = 1e-6
BIGM = 32768.0
N_CORES = 8
EMBED_WAIT = True
POOL_ENG = "dve"
_STOP = None


class _Stop(Exception):
    pass


def _stage(name):
    if _STOP == name:
        raise _Stop()


class _Op:
    __slots__ = ("eng", "emit", "deps", "is_dma", "sig", "need_sig", "idx", "waits", "knows")

    def __init__(self, eng, emit, is_dma):
        self.eng = eng
        self.emit = emit
        self.deps = []
        self.is_dma = is_dma
        self.sig = None
        self.need_sig = is_dma
        self.idx = -1


class Sched:
    ENGS = ("pe", "act", "dve", "pool", "sp")
    EPOCH = 6000
    NDMA = {"sp": 12, "pool": 6}

    def __init__(self):
        self.ops = {e: [] for e in self.ENGS}
        self.last_w = {}
        self.readers = {}
        self.dma_hist = {"sp": [], "pool": []}
        self.order = []

    def add(self, eng, emit, reads=(), writes=(), dma=False):
        op = _Op(eng, emit, dma)
        deps = []
        for k in reads:
            w = self.last_w.get(k)
            if w is not None:
                deps.append(w)
            if k[0] == "ps":
                deps.extend(r for r in self.readers.get(k, ()) if r.eng != eng)
        for k in writes:
            w = self.last_w.get(k)
            if w is not None:
                deps.append(w)
            deps.extend(self.readers.get(k, ()))
        if dma:
            h = self.dma_hist[eng]
            n = self.NDMA[eng]
            if len(h) >= n:
                deps.append(h[-n])
            h.append(op)
        seen = set()
        for d in deps:
            if d is op or id(d) in seen:
                continue
            if d.eng == "pe" and eng == "pe" and not d.is_dma:
                continue
            seen.add(id(d))
            op.deps.append(d)
            d.need_sig = True
        for k in reads:
            self.readers.setdefault(k, []).append(op)
        for k in writes:
            self.last_w[k] = op
            self.readers[k] = []
        op.idx = len(self.ops[eng])
        self.ops[eng].append(op)
        self.order.append(op)
        return op

    def lower(self, nc, es):
        sem_cache = {}

        def get_sem(name):
            if name not in sem_cache:
                sem_cache[name] = es.enter_context(nc.semaphore(name))
            return sem_cache[name]

        for e in self.ENGS:
            cnt = 0
            dcnt = 0
            for op in self.ops[e]:
                if op.is_dma:
                    n = self.NDMA[e]
                    op.sig = ("d_%s_%d" % (e, dcnt % n), 16 * (dcnt // n + 1))
                    dcnt += 1
                elif op.need_sig:
                    op.sig = ("c_%s_%d" % (e, cnt // self.EPOCH), cnt % self.EPOCH + 1)
                    cnt += 1
        for e in self.ENGS:
            for op in self.ops[e]:
                if op.sig is not None:
                    get_sem(op.sig[0])
        block = es.enter_context(nc.Block())
        dec = {"pe": block.tensor, "act": block.scalar, "dve": block.vector, "pool": block.gpsimd,
               "sp": block.sync}

        known = {e: {} for e in self.ENGS}
        for op in self.order:
            K = known[op.eng]
            need = {}
            for d in op.deps:
                sname, v = d.sig
                if K.get(sname, 0) >= v:
                    continue
                if need.get(sname, 0) < v:
                    need[sname] = v
            for d in op.deps:
                for s2, v2 in d.knows.items():
                    if K.get(s2, 0) < v2:
                        K[s2] = v2
            for sname, v in need.items():
                if K.get(sname, 0) < v:
                    K[sname] = v
            op.waits = list(need.items())
            op.knows = dict(K)
            if op.sig is not None:
                op.knows[op.sig[0]] = max(op.knows.get(op.sig[0], 0), op.sig[1])

        def make(e):
            def body(eng):
                for op in self.ops[e]:
                    items = list(op.waits)
                    emb = None
                    if EMBED_WAIT and items:
                        emb = items.pop()
                    for sname, v in items:
                        eng.wait_ge(get_sem(sname), v)
                    ins = op.emit(eng)
                    if emb is not None:
                        ins._wait_ge(get_sem(emb[0]), emb[1])
                    if op.sig is not None:
                        ins.then_inc(get_sem(op.sig[0]), 16 if op.is_dma else 1)
                if e == "sp":
                    last = {}
                    for op in self.ops[e]:
                        if op.is_dma:
                            last[op.sig[0]] = max(last.get(op.sig[0], 0), op.sig[1])
                    for sname, v in last.items():
                        eng.wait_ge(get_sem(sname), v)
            return body

        for e in self.ENGS:
            if self.ops[e]:
                dec[e](make(e))


def build_program(depth=DEPTH):
    nc = bass.Bass("TRN2", target_bir_lowering=False)
    S = Sched()

    def din(name, shape):
        return nc.dram_tensor(name, list(shape), F32, kind="ExternalInput").ap()

    def dout(name, shape):
        return nc.dram_tensor(name, list(shape), F32, kind="ExternalOutput").ap()

    x_d = din("x", (NT, D))
    cond_d = din("condT", (128, 8))
    ck_d = {"diff": din("ck_diff", (2, L_CTX, D)), "na": din("ck_na", (2, L_CTX, D))}
    cv_d = {"diff": din("cv_diff", (2, L_CTX, D)), "na": din("cv_na", (2, L_CTX, D))}
    w_ada_d = din("w_ada", (DEPTH, D, 6 * D))
    b_ada_d = din("b_adaT", (DEPTH, 128, 48))
    gmix_d = din("g_mixT", (DEPTH, 128, 8))
    gmlp_d = din("g_mlpT", (DEPTH, 128, 8))
    w_fc1_d = din("w_fc1", (DEPTH, D, 4 * D))
    w_fc2_d = din("w_fc2", (DEPTH, 4 * D, D))
    w_qkv_d = {"diff": din("w_qkv_diff", (2, D, 3 * D)), "na": din("w_qkv_na", (2, D, 3 * D))}
    w_o_d = {"diff": din("w_o_diff", (2, D, D)), "na": din("w_o_na", (2, D, D))}
    qkg_d = din("qkg", (128, 8))
    subg_d = din("subg", (128, 2))
    lamv_d = din("lamv", (64, 8))
    ident_d = din("ident", (128, 128))
    ones_d = din("ones", (128, 128))
    bones_d = din("blockones", (128, 128))
    rrope_d = din("rropeT", (128, 128))
    ecomb_d = din("ecomb", (62, 191))
    colmask_d = din("colmask", (128, 64))
    kind_d = din("kind", (64, NKEY))
    qind_d = {"diff": din("qind_diff", (64, NT)), "na": din("qind_na", (64, NT))}
    cos_d = din("cosT", (128, NT))
    sin_d = din("sinT", (128, NT))
    relt2_d = din("relT2", (2, 62, 240))

    y_d = dout("y", (NT, D))
    nk_d = {"diff": dout("nk_diff", (2, NT, D)), "na": dout("nk_na", (2, NT, D))}
    nv_d = {"diff": dout("nv_diff", (2, NT, D)), "na": dout("nv_na", (2, NT, D))}

    es = ExitStack()
    with es:
        def sb(name, shape, dt):
            return es.enter_context(nc.sbuf_tensor(name, list(shape), dt))

        xT = sb("xT", (128, 8, NT), F32)
        hT = sb("hT", (128, 8, NT), BF16)
        OT = sb("OT", (128, 8, NT), BF16)
        slabs = [sb("slab%d" % i, (128, 4096), BF16) for i in range(3)]
        PTP = [sb("PTP%d" % i, (128, 1024), BF16) for i in range(2)]
        TW = [sb("TW%d" % i, (128, 1024), F32) for i in range(3)]
        TH = [sb("TH%d" % i, (128, 512), F32) for i in range(8)]
        TH2 = [TW[0][:, 0:512], TW[0][:, 512:1024], TW[1][:, 0:512], TW[1][:, 512:1024]]
        TW2b = TW[2].bitcast(BF16)
        TB2 = [TW2b[:, 0:512], TW2b[:, 512:1024]]
        modT = [sb("modT%d" % i, (128, 48), F32) for i in range(2)]
        modv = [sb("modv%d" % i, (128, 16), F32) for i in range(2)]
        badaT = sb("badaT", (128, 48), F32)
        gmixT = sb("gmixT", (128, 8), F32)
        gmlpT = sb("gmlpT", (128, 8), F32)
        condT = sb("condT_sb", (128, 8), F32)
        scb = sb("scb", (128, 8), BF16)
        qkg = sb("qkg_sb", (128, 8), F32)
        subg = sb("subg_sb", (128, 2), F32)
        subg2 = sb("subg2_sb", (128, 2), F32)
        lamv = sb("lamv_sb", (64, 8), F32)
        lamp = sb("lamp_sb", (64, 2), F32)
        lame = sb("lame_sb", (128, 2), F32)
        neglam = sb("neglam_sb", (128, 1), F32)
        ident = sb("ident_sb", (128, 128), F32)
        identb = sb("identb_sb", (128, 128), BF16)
        onesf = sb("ones_sb", (128, 128), F32)
        onesb = sb("onesb_sb", (128, 128), BF16)
        bonesb = sb("bonesb_sb", (128, 128), BF16)
        rropeb = sb("rropeb_sb", (128, 128), BF16)
        TB = [sb("TB%d" % i, (128, 512), BF16) for i in range(3)]
        ecombb = sb("ecombb_sb", (62, 191), BF16)
        relhi = sb("relhi_sb", (62, 120), BF16)
        rello = sb("rello_sb", (62, 120), BF16)
        colmask = sb("colmask_sb", (128, 64), F32)
        relt2 = sb("relt2_sb", (62, 120), F32)
        arena = sb("arena", (128, 40960), BF16)
        arena32 = arena.bitcast(F32)
        off = 0
        qt = [[None, None], [None, None]]
        kt = [[None, None], [None, None]]
        for b in range(2):
            for s in range(2):
                qt[b][s] = arena[:, off:off + NT]
                off += NT
                kt[b][s] = arena[:, off:off + NKEY]
                off += NKEY
        Vb = arena[:, off:off + 10 * D].rearrange("p (t e) -> p t e", t=10)
        off += 10 * D
        ctok = arena[:, off:off + 2 * D].rearrange("p (t e) -> p t e", t=2)
        off += 2 * D
        assert off % 2 == 0
        off32 = off // 2
        GT = arena32[:, off32:off32 + 8 * 15 * 64]
        cosT = arena32[:, off32:off32 + NT]
        sinT = arena32[:, off32 + NT:off32 + 2 * NT]
        off32 += 8 * 15 * 64
        kst = [arena32[:, off32 + i * 512:off32 + (i + 1) * 512] for i in range(2)]
        off32 += 1024
        vst = [arena32[:, off32 + i * 512:off32 + (i + 1) * 512] for i in range(2)]
        off32 += 1024
        assert off32 * 2 <= 40960, off32
        h1T = arena[:, 0:32 * NT].rearrange("p (c t) -> p c t", c=32)
        ARENA_KEYS = []

        SS = es.enter_context(nc.psum_tensor("bankS", [128, 1024], F32))
        banks = [SS[:, 0:512], SS[:, 512:1024]] + \
            [es.enter_context(nc.psum_tensor("bank%d" % i, [128, 512], F32)) for i in range(2, 8)]

        def PS(b):
            return ("ps", b)

        def dma_sp(out, in_, reads=(), writes=()):
            return S.add("sp", lambda e, out=out, in_=in_: e.dma_start(out=out, in_=in_), reads, writes, dma=True)

        def dma_cast(out, in_, reads=(), writes=()):
            return S.add("pool", lambda e, out=out, in_=in_: e.dma_start(out=out, in_=in_), reads, writes, dma=True)

        def mm(out, lhsT, rhs, start, stop, reads, writes):
            return S.add("pe", lambda e, out=out, lhsT=lhsT, rhs=rhs, start=start, stop=stop:
                         e.matmul(out, lhsT, rhs, start=start, stop=stop), reads, writes)

        def tr(out, in_, idt, reads, writes):
            return S.add("pe", lambda e, out=out, in_=in_, idt=idt: e.transpose(out, in_, idt), reads, writes)

        def act(out, in_, func, reads, writes, bias=None, scale=None):
            kw = {}
            if bias is not None:
                kw["bias"] = bias
            if scale is not None:
                kw["scale"] = scale
            return S.add("act", lambda e, out=out, in_=in_, func=func, kw=kw:
                         e.activation(out=out, in_=in_, func=func, **kw), reads, writes)

        def stt(out, in0, scalar, in1, op0, op1, reads, writes):
            return S.add("dve", lambda e, out=out, in0=in0, scalar=scalar, in1=in1, op0=op0, op1=op1:
                         e.scalar_tensor_tensor(out=out, in0=in0, scalar=scalar, in1=in1, op0=op0, op1=op1),
                         reads, writes)

        def tt(out, in0, in1, op, reads, writes):
            return S.add("dve", lambda e, out=out, in0=in0, in1=in1, op=op:
                         e.tensor_tensor(out=out, in0=in0, in1=in1, op=op), reads, writes)

        def ts(out, in0, s1, s2, op0, op1, reads, writes):
            if s2 is None:
                return S.add("dve", lambda e, out=out, in0=in0, s1=s1, op0=op0:
                             e.tensor_scalar(out, in0, s1, None, op0=op0), reads, writes)
            return S.add("dve", lambda e, out=out, in0=in0, s1=s1, s2=s2, op0=op0, op1=op1:
                         e.tensor_scalar(out, in0, s1, s2, op0=op0, op1=op1), reads, writes)

        def vcopy(out, in_, reads, writes):
            return S.add("dve", lambda e, out=out, in_=in_: e.tensor_copy(out=out, in_=in_), reads, writes)

        def ptt(out, in0, in1, op, reads, writes):
            return S.add(POOL_ENG, lambda e, out=out, in0=in0, in1=in1, op=op:
                         e.tensor_tensor(out=out, in0=in0, in1=in1, op=op), reads, writes)

        def pcopy(out, in_, reads, writes):
            return S.add(POOL_ENG, lambda e, out=out, in_=in_: e.tensor_copy(out=out, in_=in_), reads, writes)

        def recip(out, in_, reads, writes):
            return S.add("dve", lambda e, out=out, in_=in_: e.reciprocal(out=out, in_=in_), reads, writes)

        slab_ctr = [0]

        def next_slab():
            i = slab_ctr[0] % 3
            slab_ctr[0] += 1
            return i

        def load_slab_k8(w2d, col0, width, col_off=0, si=None):
            if si is None:
                si = next_slab()
            return si

        C = "const"
        dma_sp(ident[:], ident_d, (), [("c", "ident")])
        dma_cast(identb[:], ident_d, (), [("c", "identb")])
        dma_sp(onesf[:], ones_d, (), [("c", "ones")])
        dma_cast(onesb[:], ones_d, (), [("c", "onesb")])
        dma_cast(bonesb[:], bones_d, (), [("c", "bones")])
        dma_cast(rropeb[:], rrope_d, (), [("c", "rrope")])
        dma_cast(ecombb[:], ecomb_d, (), [("c", "ecomb")])
        dma_sp(colmask[:], colmask_d, (), [("c", "colmask")])
        dma_sp(condT[:], cond_d, (), [("c", "cond")])
        dma_sp(qkg[:], qkg_d, (), [("c", "qkg")])
        dma_sp(subg[:], subg_d, (), [("c", "subg")])
        dma_sp(lamv[:], lamv_d, (), [("c", "lamv")])
        act(scb[:], condT[:], AF.Silu, [("c", "cond")], [("c", "scb")])
        for b in range(2):
            dma_cast(kt[b][0][64:128, :], kind_d, (), [("kt", b, 0, "ind")])
            dma_cast(kt[b][1][0:64, :], kind_d, (), [("kt", b, 1, "ind")])

        for t in range(8):
            tw = TW[t % 2]
            twk = ("TW", t % 2)
            dma_sp(tw[:], x_d[t * 128:(t + 1) * 128, :], (), [twk])
            for g in range(2):
                bk = (2 * t + g) % 4
                for cc in range(4):
                    c = g * 4 + cc
                    tr(banks[bk][:, cc * 128:(cc + 1) * 128], tw[:, c * 128:(c + 1) * 128], ident[:],
                       [twk, ("c", "ident")], [PS(bk)])
                S.add("act", lambda e, bk=bk, g=g, t=t: e.activation(
                    out=xT[:, g * 4:(g + 1) * 4, t * 128:(t + 1) * 128],
                    in_=banks[bk][:, :].rearrange("p (c n) -> p c n", c=4), func=AF.Copy),
                    [PS(bk)], [("xT", c2) for c2 in range(g * 4, g * 4 + 4)])

        def ada_steps(l):
            steps = []
            par = l % 2
            for s in range(12):
                def step(s=s):
                    if s == 0:
                        dma_sp(badaT[:], b_ada_d[l], (), [("c", "bada")])
                    si = next_slab()
                    sl = slabs[si]
                    dma_cast(sl[:, 0:4096].rearrange("p (k n) -> p k n", k=8),
                             w_ada_d[l][:, s * 512:(s + 1) * 512].rearrange("(k p) n -> p k n", p=128),
                             (), [("slab", si)])
                    for m in range(4):
                        col = s * 4 + m
                        for kc in range(8):
                            mm(banks[6][:, col:col + 1], sl[:, kc * 512 + m * 128: kc * 512 + (m + 1) * 128],
                               scb[:, kc:kc + 1], kc == 0, kc == 7,
                               [("slab", si), ("c", "scb")], [PS(6)])
                    tt(modT[par][:, s * 4:(s + 1) * 4], banks[6][:, s * 4:(s + 1) * 4], badaT[:, s * 4:(s + 1) * 4],
                       ALU.add, [PS(6), ("c", "bada")], [("mod", par, s)])
                steps.append(step)
            return steps

        def mod_derive(l, which):
            par = l % 2
            if which == 0:
                dma_sp(gmixT[:], gmix_d[l], (), [("c", "gmix")])
                stt(modv[par][:, 0:8], modT[par][:, 8:16], 1.0, gmixT[:], ALU.add, ALU.mult,
                    [("mod", par, 2), ("mod", par, 3), ("c", "gmix")], [("modv", par, 0)])
            else:
                dma_sp(gmlpT[:], gmlp_d[l], (), [("c", "gmlp")])
                stt(modv[par][:, 8:16], modT[par][:, 32:40], 1.0, gmlpT[:], ALU.add, ALU.mult,
                    [("mod", par, 8), ("mod", par, 9), ("c", "gmlp")], [("modv", par, 1)])

        ada_queue = []

        def pump_ada(n=1):
            for _ in range(n):
                if ada_queue:
                    ada_queue.pop(0)()

        def norm_sq(c):
            for hf in range(2):
                act(TB[hf][:], xT[:, c, hf * 512:(hf + 1) * 512], AF.Square, [("xT", c)], [("TB", hf)])

        def norm_ss(c):
            for hf in range(2):
                mm(banks[4 + hf][:], onesb[:], TB[hf][:], c == 0, c == 7,
                   [("TB", hf), ("c", "onesb")], [PS(4 + hf)])

        def norm_stats(c):
            norm_sq(c)
            norm_ss(c)

        def norm_stats_delayed(m):
            if m > 0:
                norm_ss(m - 1)
            norm_sq(m)
            if m == 7:
                norm_ss(7)

        def norm_mod(l, which, have_stats=False):
            par = l % 2
            gs = modv[par][:, 0:8] if which == 0 else modv[par][:, 8:16]
            shift = modT[par][:, 0:8] if which == 0 else modT[par][:, 24:32]
            mk = [("mod", par, 0 + 6 * which), ("mod", par, 1 + 6 * which), ("modv", par, which)]
            if not have_stats:
                for c in range(8):
                    norm_stats(c)
            rs = TW[2]
            for hf in range(2):
                act(rs[:, hf * 512:(hf + 1) * 512], banks[4 + hf][:], AF.Ln, [PS(4 + hf)], [("TW", 2, hf)],
                    bias=EPS, scale=1.0 / D)
                act(rs[:, hf * 512:(hf + 1) * 512], rs[:, hf * 512:(hf + 1) * 512], AF.Exp, [("TW", 2, hf)],
                    [("TW", 2, hf)], scale=-0.5)
            for c in range(8):
                tw = TW[c % 2]
                twk = ("TW", c % 2)
                tt(tw[:], xT[:, c, :], rs[:], ALU.mult, [("xT", c), ("TW", 2, 0), ("TW", 2, 1)], [twk])
                act(hT[:, c, :], tw[:], AF.Identity, [twk] + mk, [("hT", c)], bias=shift[:, c:c + 1],
                    scale=gs[:, c:c + 1])

        def attention_layer(l):
            T = "diff" if l % 2 == 0 else "na"
            i = l // 2
            par = l % 2
            wq = w_qkv_d[T][i]
            lam_init = 0.8 - 0.6 * math.exp(-0.3 * l)
            gate = modT[par][:, 16:24]
            qg = qkg[:, (0 if T == "diff" else 4) + i:(0 if T == "diff" else 4) + i + 1]
            kg = qkg[:, (2 if T == "diff" else 6) + i:(2 if T == "diff" else 6) + i + 1]

            norm_mod(l, 0, have_stats=(l > 0))
            _stage('norm')

            for b in range(2):
                dma_cast(qt[b][0][64:128, :], qind_d[T], (), [("qt", b, 0, "ind")])
                dma_cast(qt[b][1][0:64, :], qind_d[T], (), [("qt", b, 1, "ind")])
            dma_cast(ctok[:, :, :], ck_d[T][i].rearrange("(t p) e -> p t e", p=128), (), [("ctok",)])
            dma_cast(Vb[:, 8:10, :], cv_d[T][i].rearrange("(t p) e -> p t e", p=128), (), [("V", 8), ("V", 9)])

            if T == "diff":
                dma_sp(cosT, cos_d, (), [("GT",)])
                dma_sp(sinT, sin_d, (), [("GT",)])
                tt(lamp[:, 0:1], lamv[:, i * 4:i * 4 + 1], lamv[:, i * 4 + 1:i * 4 + 2], ALU.mult,
                   [("c", "lamv")], [("lamp",)])
                tt(lamp[:, 1:2], lamv[:, i * 4 + 2:i * 4 + 3], lamv[:, i * 4 + 3:i * 4 + 4], ALU.mult,
                   [("c", "lamv"), ("lamp",)], [("lamp",)])
                mm(banks[7][:, 0:2], onesf[0:64, :], lamp[:, :], True, True, [("lamp",), ("c", "ones")], [PS(7)])
                act(lame[:], banks[7][:, 0:2], AF.Exp, [PS(7)], [("lame",)])
                stt(neglam[:], lame[:, 1:2], -lam_init, lame[:, 0:1], ALU.add, ALU.subtract,
                    [("lame",)], [("neglam",)])
                ts(subg2[:, i:i + 1], subg[:, i:i + 1], 1.0 - lam_init, None, ALU.mult, None,
                   [("c", "subg")], [("subg2", i)])

            _stage('pre')
            def v_projection():
                vctr = 0
                for s in range(2):
                    si = next_slab()
                    sl = slabs[si]
                    dma_cast(sl[:, 0:4096].rearrange("p (k n) -> p k n", k=8),
                             wq[:, 2048 + s * 512:2048 + (s + 1) * 512].rearrange("(k p) n -> p k n", p=128),
                             (), [("slab", si)])
                    for t in range(8):
                        bk = vctr % 4
                        for kc in range(8):
                            mm(banks[bk][:], hT[:, kc, t * 128:(t + 1) * 128], sl[:, kc * 512:(kc + 1) * 512],
                               kc == 0, kc == 7, [("slab", si), ("hT", kc)], [PS(bk)])
                        st = vst[vctr % 2]
                        stk = ("vst", vctr % 2)
                        act(st, banks[bk][:], AF.Copy, [PS(bk)], [stk])
                        vcopy(Vb[:, t, s * 512:(s + 1) * 512], banks[bk][:], [PS(bk)], [("V", t)])
                        dma_sp(nv_d[T][i][t * 128:(t + 1) * 128, s * 512:(s + 1) * 512], st, [stk], ())
                        vctr += 1
                        for _ in range(2):
                            step(bg_lo)
                            step(bg_k)

            _stage('vproj')
            def build_gtable(hg):
                dma_sp(relt2[:], relt2_d[i][:, hg * 120:(hg + 1) * 120], (), [("c", "relt2")])
                vcopy(relhi[:], relt2[:], [("c", "relt2")], [("c", "relhi")])
                tt(rello[:], relt2[:], relhi[:], ALU.subtract, [("c", "relt2"), ("c", "relhi")], [("c", "rello")])
                for qc in range(64):
                    bk = 6 + (qc % 2)
                    mm(banks[bk][:, 0:120], ecombb[:, 63 - qc:191 - qc], relhi[:, :],
                       True, False, [("c", "ecomb"), ("c", "relhi")], [PS(bk)])
                    mm(banks[bk][:, 0:120], ecombb[:, 63 - qc:191 - qc], rello[:, :],
                       False, True, [("c", "ecomb"), ("c", "rello")], [PS(bk)])
                    S.add("dve", lambda e, bk=bk, qc=qc: e.tensor_scalar(
                        GT.rearrange("p (a q) -> p a q", q=64)[:, :, qc], banks[bk][:, 0:120],
                        8.0, colmask[:, qc:qc + 1], op0=ALU.mult, op1=ALU.add),
                        [PS(bk), ("c", "colmask")], [("GT",)])

            def prep_load(c):
                si = next_slab()
                sl = slabs[si]
                dma_cast(sl[:, 0:1024].rearrange("p (k n) -> p k n", k=8),
                         wq[:, c * 128:(c + 1) * 128].rearrange("(k p) n -> p k n", p=128),
                         (), [("slab", si)])
                dma_cast(sl[:, 1024:2048].rearrange("p (k n) -> p k n", k=8),
                         wq[:, 1024 + c * 128:1024 + (c + 1) * 128].rearrange("(k p) n -> p k n", p=128),
                         (), [("slab", si)])
                return si

            def prep(c, X, si):
                b = c % 2
                sl = slabs[si]
                bk = 6 + X
                if X == 0:
                    tq, tsq, trs, tn = TH[0], TH[1], TH[2], TH[3]
                    tb0, tb1 = TB[0], TB[1]
                    kq, ksq, krs, kn, kb0, kb1 = ("TH", 0), ("TH", 1), ("TH", 2), ("TH", 3), ("TB", 0), ("TB", 1)
                else:
                    tq, tsq, trs, tn = TH2
                    tb0, tb1 = TB2[0], TB2[1]
                    kq, ksq, krs, kn, kb0, kb1 = ("TH2", 0), ("TH2", 1), ("TH2", 2), ("TH2", 3), ("TB2", 0), ("TB2", 1)
                    for j in range(2):
                        mm(banks[bk][:, j * 128:(j + 1) * 128], ctok[:, j, c * 128:(c + 1) * 128], identb[:],
                           True, True, [("ctok",), ("c", "identb")], [PS(bk)])
                    yield
                    vcopy(kt[b][0][0:64, NT:NKEY], banks[bk][0:64, 0:256], [PS(bk)], [("kt", b, 0, "ctx")])
                    vcopy(kt[b][1][64:128, NT:NKEY], banks[bk][64:128, 0:256], [PS(bk)],
                          [("kt", b, 1, "ctx")])
                    yield
                g = qg if X == 0 else kg
                dst = qt if X == 0 else kt
                nm = "qt" if X == 0 else "kt"
                for hf in range(2):
                    cs = slice(hf * 512, (hf + 1) * 512)
                    for kc in range(8):
                        mm(banks[bk][:], sl[:, X * 1024 + kc * 128:X * 1024 + (kc + 1) * 128], hT[:, kc, cs],
                           kc == 0, kc == 7, [("slab", si), ("hT", kc)], [PS(bk)])
                        if kc % 2 == 1:
                            yield
                    vcopy(tq[:], banks[bk][:], [PS(bk)], [kq])
                    yield
                    tt(tb0[:], tq[:], tq[:], ALU.mult, [kq], [kb0])
                    yield
                    mm(banks[bk][:], bonesb[:], tb0[:], True, True, [kb0, ("c", "bones")], [PS(bk)])
                    yield
                    act(trs[:], banks[bk][:], AF.Ln, [PS(bk)], [krs], bias=EPS, scale=1.0 / 64)
                    yield
                    act(trs[:], trs[:], AF.Exp, [krs], [krs], scale=-0.5)
                    yield
                    stt(tn[:], tq[:], g, trs[:], ALU.mult, ALU.mult, [kq, krs, ("c", "qkg")], [kn])
                    fin = tn
                    fink = kn
                    yield
                    if T == "diff":
                        vcopy(tb1[:], tn[:], [kn], [kb1])
                        tt(tq[:], tn[:], cosT[:, cs], ALU.mult, [kn, ("GT",)], [kq])
                        yield
                        mm(banks[bk][:], rropeb[:], tb1[:], True, True, [kb1, ("c", "rrope")], [PS(bk)])
                        yield
                        tt(tsq[:], banks[bk][:], sinT[:, cs], ALU.mult, [PS(bk), ("GT",)], [ksq])
                        yield
                        tt(trs[:], tq[:], tsq[:], ALU.add, [kq, ksq], [krs])
                        fin = trs
                        fink = krs
                        yield
                    vcopy(dst[b][0][0:64, cs], fin[0:64, :], [fink], [(nm, b, 0, "lat", hf)])
                    vcopy(dst[b][1][64:128, cs], fin[64:128, :], [fink], [(nm, b, 1, "lat", hf)])
                    if X == 1:
                        for tb in range(4):
                            tr(banks[bk][:, tb * 128:(tb + 1) * 128], fin[:, tb * 128:(tb + 1) * 128], ident[:],
                               [fink, ("c", "ident")], [PS(bk)])
                        yield
                        ks = kst[hf]
                        vcopy(ks, banks[bk][:], [PS(bk)], [("kst", hf)])
                        dma_sp(nk_d[T][i][hf * 512:(hf + 1) * 512, c * 128:(c + 1) * 128]
                               .rearrange("(t p) e -> p t e", p=128),
                               ks.rearrange("p (t e) -> p t e", t=4), [("kst", hf)], ())
                    yield

            def epilogue(c, hq, s, bO, bZ):
                qs = slice(hq * 512, (hq + 1) * 512)
                rz = TH[4]
                act(rz[:], banks[bZ][:], AF.Ln, [PS(bZ)], [("TH", 4)])
                yield
                act(rz[:], rz[:], AF.Exp, [("TH", 4)], [("TH", 4)], scale=-1.0)
                yield
                if T == "na":
                    ps_ = slice(0, 64) if s == 0 else slice(64, 128)
                    tt(OT[ps_, c, qs], banks[bO][ps_, :], rz[ps_, :], ALU.mult, [PS(bO), ("TH", 4)],
                       [("OT", c, hq, s)])
                    return
                if s == 0:
                    tt(TH[5][:], banks[bO][:], rz[:], ALU.mult, [PS(bO), ("TH", 4)], [("TH", 5)])
                    return
                tt(TH[6][:], banks[bO][:], rz[:], ALU.mult, [PS(bO), ("TH", 4)], [("TH", 6)])
                yield
                stt(TH[7][:], TH[6][:], neglam[:, 0:1], TH[5][:], ALU.mult, ALU.add,
                    [("TH", 5), ("TH", 6), ("neglam",)], [("TH", 7)])
                yield
                tt(TB[2][:], TH[7][:], TH[7][:], ALU.mult, [("TH", 7)], [("TB", 2)])
                yield
                mm(banks[bZ][:], onesb[:], TB[2][:], True, True, [("TB", 2), ("c", "onesb")], [PS(bZ)])
                yield
                act(TH[5][:], banks[bZ][:], AF.Ln, [PS(bZ)], [("TH", 5)], bias=EPS, scale=1.0 / 128)
                yield
                act(TH[5][:], TH[5][:], AF.Exp, [("TH", 5)], [("TH", 5)], scale=-0.5)
                yield
                stt(OT[:, c, qs], TH[7][:], subg2[:, i:i + 1], TH[5][:], ALU.mult, ALU.mult,
                    [("TH", 7), ("TH", 5), ("subg2", i)], [("OT", c, hq, 0), ("OT", c, hq, 1)])

            bg_hi = []
            bg_lo = []

            def step(q):
                while q:
                    try:
                        next(q[0])
                        return True
                    except StopIteration:
                        q.pop(0)
                return False

            def drain(q, keep=0):
                while len(q) > keep:
                    g0 = q[0]
                    for _ in g0:
                        pass
                    q.pop(0)

            bg_k = []

            def pump():
                step(bg_hi)
                step(bg_lo)
                step(bg_k)

            pt_ctr = [0]
            st_ctr = [0]
            acc_ctr = [0]

            streams = {}
            seq = []
            for c in range(8):
                for hq in range(2):
                    if T == "diff":
                        blocks = list(range(10))
                    else:
                        blocks = ([0, 1, 2, 3, 4, 5] if hq == 0 else [2, 3, 4, 5, 6, 7]) + [8, 9]
                    for s in range(2):
                        k = (c, hq, s)
                        streams[k] = blocks
                        seq += [(k, p) for p in range(len(blocks) // 2)]
            pis = {}
            acc = {}
            started = set()

            def finish_prep():
                while bg_lo or bg_k:
                    step(bg_lo)
                    step(bg_k)

            def ensure_chunk(c):
                if c in started:
                    return
                started.add(c)
                finish_prep()
                if T == "na" and c == 4:
                    build_gtable(1)
                if c + 1 < 8:
                    si1 = prep_load(c + 1)
                    bg_lo.append(prep(c + 1, 0, si1))
                    bg_k.append(prep(c + 1, 1, si1))

            def issue_s_pair(k, p):
                c, hq, s = k
                ensure_chunk(c)
                b = c % 2
                blocks = streams[k]
                qs = slice(hq * 512, (hq + 1) * 512)
                for h in range(2):
                    j = blocks[2 * p + h]
                    kr = [("kt", b, s, "ind"), ("qt", b, s, "ind"), ("qt", b, s, "lat", hq)]
                    kr.append(("kt", b, s, "ctx") if j >= 8 else ("kt", b, s, "lat", j // 4))
                    mm(banks[h], kt[b][s][:, j * 128:(j + 1) * 128], qt[b][s][:, qs], True, True,
                       kr, [PS(h)])
                    if T == "na" and j < 8:
                        a = 2 * j
                        lo, hi = (0, a + 5) if a <= 6 else (a - 3, 15)
                        lo = max(lo, 8 * hq)
                        hi = min(hi, 8 * hq + 7)
                        hl = (2 * c + s) % 8
                        sp0 = 7 - a + lo
                        n = (hi - lo + 1) * 64
                        go = (hl * 15 + sp0) * 64
                        po = h * 512 + (lo - 8 * hq) * 64
                        tt(SS[:, po:po + n], SS[:, po:po + n], GT[:, go:go + n], ALU.add,
                           [PS(h), ("GT",)], [PS(h)])
                pi = pt_ctr[0] % 2
                pt_ctr[0] += 1
                act(PTP[pi][:], SS[:, :], AF.Exp, [PS(0), PS(1)], [("PT", pi)], scale=0.125)
                pis[(k, p)] = pi

            def issue_av_pair(k, p):
                c, hq, s = k
                blocks = streams[k]
                nb = len(blocks)
                if p == 0:
                    drain(bg_hi, keep=1)
                    aset = acc_ctr[0] % 2
                    acc_ctr[0] += 1
                    acc[k] = (2 + 2 * aset, 3 + 2 * aset)
                bO, bZ = acc[k]
                pi = pis[(k, p)]
                for h in range(2):
                    bi = 2 * p + h
                    j = blocks[bi]
                    rhs = PTP[pi][:, h * 512:(h + 1) * 512]
                    mm(banks[bO][:], Vb[:, j, c * 128:(c + 1) * 128], rhs, bi == 0, bi == nb - 1,
                       [("PT", pi), ("V", j)], [PS(bO)])
                    mm(banks[bZ][:], onesb[:], rhs, bi == 0, bi == nb - 1,
                       [("PT", pi), ("c", "onesb")], [PS(bZ)])
                if p == nb // 2 - 1:
                    bg_hi.append(epilogue(c, hq, s, bO, bZ))

            if T == "na":
                build_gtable(0)
            arena_fence(False)
            si0 = prep_load(0)
            bg_lo.append(prep(0, 0, si0))
            bg_k.append(prep(0, 1, si0))
            v_projection()
            issue_s_pair(*seq[0])
            for g, (k, p) in enumerate(seq):
                if g + 1 < len(seq):
                    issue_s_pair(*seq[g + 1])
                issue_av_pair(k, p)
                pump()
                pump()
            finish_prep()
            drain(bg_hi)

            _stage('attn')
            if l == 0:
                pump_ada(2)
            octr = 0
            for s in range(2):
                si = next_slab()
                sl = slabs[si]
                dma_cast(sl[:, 0:4096].rearrange("p (k n) -> p k n", k=8),
                         w_o_d[T][i][:, s * 512:(s + 1) * 512].rearrange("(k p) n -> p k n", p=128),
                         (), [("slab", si)])
                for m4 in range(4):
                    m = s * 4 + m4
                    for hf in range(2):
                        bk = octr % 4
                        octr += 1
                        cs = slice(hf * 512, (hf + 1) * 512)
                        for kc in range(8):
                            mm(banks[bk][:], sl[:, kc * 512 + m4 * 128:kc * 512 + (m4 + 1) * 128], OT[:, kc, cs],
                               kc == 0, kc == 7,
                               [("slab", si), ("OT", kc, hf, 0), ("OT", kc, hf, 1)], [PS(bk)])
                        stt(xT[:, m, cs], banks[bk][:], gate[:, m:m + 1], xT[:, m, cs], ALU.mult, ALU.add,
                            [PS(bk), ("xT", m), ("mod", par, 4), ("mod", par, 5)], [("xT", m)])
                    norm_stats_delayed(m)

        def mlp_layer(l):
            par = l % 2
            gate = modT[par][:, 40:48]
            _stage('oproj')
            if l == 0:
                pump_ada(4)
            mod_derive(l, 1)
            norm_mod(l, 1, have_stats=True)
            _stage('norm2')
            ctr = 0
            for s in range(8):
                si = next_slab()
                sl = slabs[si]
                dma_cast(sl[:, 0:4096].rearrange("p (k n) -> p k n", k=8),
                         w_fc1_d[l][:, s * 512:(s + 1) * 512].rearrange("(k p) n -> p k n", p=128),
                         (), [("slab", si)])
                for m4 in range(4):
                    f = s * 4 + m4
                    for hf in range(2):
                        bk = ctr % 4
                        cs = slice(hf * 512, (hf + 1) * 512)
                        for kc in range(8):
                            xr = [("ev", "fc1", l, ctr - 3)] if (kc == 0 and ctr % 2 == 0 and ctr >= 4) else []
                            mm(banks[bk][:], sl[:, kc * 512 + m4 * 128:kc * 512 + (m4 + 1) * 128], hT[:, kc, cs],
                               kc == 0, kc == 7, [("slab", si), ("hT", kc)] + xr, [PS(bk)])
                        th = TH[ctr % 4]
                        thk = ("TH", ctr % 4)
                        act(th[:], banks[bk][:], AF.Relu, [PS(bk)], [thk, ("ev", "fc1", l, ctr)])
                        tt(h1T[:, f, cs], th[:], th[:], ALU.mult, [thk], [("h1T", f, hf)])
                        ctr += 1
                pump_ada(1)
            _stage('fc1')
            for m in range(8):
                si = next_slab()
                sl = slabs[si]
                dma_cast(sl[:, 0:4096].rearrange("p (k n) -> p k n", k=32),
                         w_fc2_d[l][:, m * 128:(m + 1) * 128].rearrange("(k p) n -> p k n", p=128),
                         (), [("slab", si)])
                for hf in range(2):
                    bk = ctr % 4
                    ctr += 1
                    cs = slice(hf * 512, (hf + 1) * 512)
                    g2 = 2 * m + hf
                    for kc in range(32):
                        xr = [("ev", "fc2", l, g2 - 3)] if (kc == 0 and g2 % 2 == 0 and g2 >= 4) else []
                        mm(banks[bk][:], sl[:, kc * 128:(kc + 1) * 128], h1T[:, kc, cs], kc == 0, kc == 31,
                           [("slab", si), ("h1T", kc, hf)] + xr, [PS(bk)])
                    stt(xT[:, m, cs], banks[bk][:], gate[:, m:m + 1], xT[:, m, cs], ALU.mult, ALU.add,
                        [PS(bk), ("xT", m), ("mod", par, 10), ("mod", par, 11)], [("xT", m), ("ev", "fc2", l, g2)])
                if l + 1 < depth:
                    norm_stats_delayed(m)
                pump_ada(1)

        arena_att_keys = []
        for b in range(2):
            for s in range(2):
                arena_att_keys += [("qt", b, s, "ind"), ("qt", b, s, "lat", 0), ("qt", b, s, "lat", 1),
                                   ("kt", b, s, "ind"), ("kt", b, s, "ctx"), ("kt", b, s, "lat", 0),
                                   ("kt", b, s, "lat", 1)]
        arena_att_keys += [("V", t) for t in range(10)] + [("ctok",), ("GT",)]
        arena_att_keys += [("kst", 0), ("kst", 1), ("vst", 0), ("vst", 1)]
        arena_mlp_keys = [("h1T", f, hf) for f in range(32) for hf in range(2)]

        def arena_fence(to_mlp):
            keys = arena_att_keys + arena_mlp_keys + [("TW", 0), ("TW", 1), ("TW", 2, 0), ("TW", 2, 1)] + \
                [("TH2", j) for j in range(4)] + [("TB2", 0), ("TB2", 1)]
            S.add("dve", lambda e: e.tensor_copy(out=lamp[:, 0:1], in_=lamp[:, 0:1]), keys, keys + [("lamp",)])

        ada_queue.extend(ada_steps(0))
        pump_ada(4)
        try:
            for l in range(depth):
                mod_derive(l, 0)
                if l > 0:
                    arena_fence(False)
                    for b in range(2):
                        dma_cast(kt[b][0][64:128, :], kind_d, (), [("kt", b, 0, "ind")])
                        dma_cast(kt[b][1][0:64, :], kind_d, (), [("kt", b, 1, "ind")])
                attention_layer(l)
                arena_fence(True)
                if l + 1 < depth:
                    ada_queue.extend(ada_steps(l + 1))
                mlp_layer(l)
                pump_ada(len(ada_queue))
        except _Stop:
            pass

        for t in range(8):
            tw = TW[t % 2]
            twk = ("TW", t % 2)
            for g in range(2):
                bk = (2 * t + g) % 4
                for cc in range(4):
                    c = g * 4 + cc
                    tr(banks[bk][:, cc * 128:(cc + 1) * 128], xT[:, c, t * 128:(t + 1) * 128], ident[:],
                       [("xT", c), ("c", "ident")], [PS(bk)])
                act(tw[:, g * 512:(g + 1) * 512], banks[bk][:], AF.Copy, [PS(bk)], [twk])
            dma_sp(y_d[t * 128:(t + 1) * 128, :], tw[:], [twk], ())

        S.lower(nc, es)
    return nc


def _const_tables():
    ident = np.eye(128, dtype=np.float32)
    ones = np.ones((128, 128), np.float32)
    bones = np.zeros((128, 128), np.float32)
    bones[:64, :64] = 1.0
    bones[64:, 64:] = 1.0
    rr = np.zeros((128, 128), np.float32)
    for m in range(128):
        d = m % 64
        if (d % 32) < 16:
            rr[m + 16, m] = -1.0
        else:
            rr[m - 16, m] = 1.0
    ecomb = np.zeros((62, 191), np.float32)
    for j in range(31):
        ecomb[j, j + 48] = 1.0
        ecomb[31 + j, j + 112] = 1.0
    return ident, ones, bones, rr, ecomb


def _core_tables(is_sample):
    rows = np.arange(NT) // 64
    cols = np.arange(NT) % 64
    kind = np.zeros((64, NKEY), np.float32)
    kind[rows, np.arange(NT)] = 1.0
    kind[16, NT:] = 1.0
    kind[17, :] = 1.0

    def qind(win, ctxvis):
        q = np.zeros((64, NT), np.float32)
        q[0:16, :] = BIGM * win[:, rows]
        q[16, :] = BIGM * ctxvis
        q[17, :] = -BIGM
        return q

    r = np.arange(16)
    if is_sample:
        win_diff = np.ones((16, 16), np.float32)
        rs = np.clip(r - 4, 0, 8)
        win_na = ((r[:, None] >= rs[None, :]) & (r[:, None] < rs[None, :] + 8)).astype(np.float32)
        ctxvis = 1.0
        kc = np.arange(64)[:, None]
        qc = np.arange(64)[None, :]
        cs = np.clip(qc - 8, 0, 48)
        valid = (kc >= cs) & (kc < cs + 16)
        colmask = np.where(valid, 0.0, -BIGM).astype(np.float32)
        colmask = np.concatenate([colmask, colmask], axis=0)
        n_freq = 16
        freqs = (np.float32(10000.0) ** (-np.arange(n_freq, dtype=np.float32) / np.float32(n_freq))).astype(np.float32)
        ang_row = rows.astype(np.float32)[None, :] * freqs[:, None]
        ang_col = cols.astype(np.float32)[None, :] * freqs[:, None]
        ang = np.zeros((128, NT), np.float32)
        for p in range(128):
            d = p % 64
            ang[p] = ang_row[d % 16] if d < 32 else ang_col[d % 16]
        cosT = np.cos(ang).astype(np.float32)
        sinT = np.sin(ang).astype(np.float32)
    else:
        same = (r[:, None] // 4 == r[None, :] // 4).astype(np.float32)
        win_diff = same
        win_na = same
        ctxvis = 0.0
        colmask = np.zeros((128, 64), np.float32)
        cosT = np.ones((128, NT), np.float32)
        sinT = np.zeros((128, NT), np.float32)
    return kind, qind(win_diff, ctxvis), qind(win_na, ctxvis), colmask, cosT, sinT


_NC_CACHE = {}
_RETURN_IN_MAPS = False
_DEPTH_RUN = DEPTH


def kernel(x_prompt, x_sample, cache_diff_k, cache_diff_v, cache_na_k, cache_na_v, c, c_ctx,
           w_ada, b_ada, norm_mix_g, norm_mlp_g, w_fc1, w_fc2,
           w_qkv_diff, w_o_diff, q_norm_diff_g, k_norm_diff_g,
           lambda_q1, lambda_k1, lambda_q2, lambda_k2, subln_g,
           w_qkv_na, w_o_na, q_norm_na_g, k_norm_na_g, rel_bias_na):
    f = lambda a: np.ascontiguousarray(np.asarray(a, dtype=np.float32))
    x_prompt, x_sample = f(x_prompt), f(x_sample)
    cache_diff_k, cache_diff_v, cache_na_k, cache_na_v = map(f, (cache_diff_k, cache_diff_v, cache_na_k, cache_na_v))
    c, c_ctx = f(c), f(c_ctx)
    ident, ones, bones, rr, ecomb = _const_tables()

    def colT(v, ncol):
        return np.ascontiguousarray(np.asarray(v, np.float32).reshape(ncol, 128).T)

    b_adaT = np.stack([colT(b_ada[l], 48) for l in range(DEPTH)])
    g_mixT = np.stack([colT(norm_mix_g[l], 8) for l in range(DEPTH)])
    g_mlpT = np.stack([colT(norm_mlp_g[l], 8) for l in range(DEPTH)])
    rep2 = lambda v: np.concatenate([np.asarray(v, np.float32)] * 2)
    qkg = np.stack([rep2(q_norm_diff_g[0]), rep2(q_norm_diff_g[1]), rep2(k_norm_diff_g[0]), rep2(k_norm_diff_g[1]),
                    rep2(q_norm_na_g[0]), rep2(q_norm_na_g[1]), rep2(k_norm_na_g[0]), rep2(k_norm_na_g[1])], axis=1)
    subg = np.stack([np.asarray(subln_g[0], np.float32), np.asarray(subln_g[1], np.float32)], axis=1)
    lamv = np.stack([np.asarray(v[i], np.float32) for i in range(2)
                     for v in (lambda_q1, lambda_k1, lambda_q2, lambda_k2)], axis=1)
    rel = np.asarray(rel_bias_na, np.float32)
    relT2 = np.zeros((2, 62, 240), np.float32)
    for half in range(2):
        for sp in range(15):
            dr = 14 - sp + half
            if dr > 14:
                continue
            relT2[:, half * 31:(half + 1) * 31, sp::15] = np.transpose(rel[:, :, dr, :], (0, 2, 1))
    shared = {
        "w_ada": f(w_ada), "b_adaT": f(b_adaT), "g_mixT": f(g_mixT), "g_mlpT": f(g_mlpT),
        "w_fc1": f(w_fc1), "w_fc2": f(w_fc2), "w_qkv_diff": f(w_qkv_diff), "w_qkv_na": f(w_qkv_na),
        "w_o_diff": f(w_o_diff), "w_o_na": f(w_o_na), "qkg": f(qkg), "subg": f(subg), "lamv": f(lamv),
        "ident": ident, "ones": ones, "blockones": bones, "rropeT": rr, "ecomb": ecomb, "relT2": relT2,
    }
    tabs = {True: _core_tables(True), False: _core_tables(False)}
    zc = np.zeros((2, L_CTX, D), np.float32)
    in_maps = []
    for core in range(N_CORES):
        is_sample = core < 4
        kind, qd, qn, colmask, cosT, sinT = tabs[is_sample]
        m = dict(shared)
        m.update({"kind": kind, "qind_diff": qd, "qind_na": qn, "colmask": colmask, "cosT": cosT, "sinT": sinT})
        if is_sample:
            bi = core
            m["x"] = x_sample[bi]
            m["condT"] = colT(c[bi], 8)
            m["ck_diff"] = f(cache_diff_k[bi].reshape(2, L_CTX, D))
            m["cv_diff"] = f(cache_diff_v[bi].reshape(2, L_CTX, D))
            m["ck_na"] = f(cache_na_k[bi].reshape(2, L_CTX, D))
            m["cv_na"] = f(cache_na_v[bi].reshape(2, L_CTX, D))
        else:
            p0 = (core - 4) * 4
            m["x"] = f(x_prompt[p0:p0 + 4].reshape(NT, D))
            m["condT"] = colT(c_ctx, 8)
            m["ck_diff"] = zc
            m["cv_diff"] = zc
            m["ck_na"] = zc
            m["cv_na"] = zc
        in_maps.append(m)

    if _RETURN_IN_MAPS:
        return in_maps
    if "nc" not in _NC_CACHE:
        _NC_CACHE["nc"] = build_program(_DEPTH_RUN)
    nc = _NC_CACHE["nc"]
    res = run_bass_kernel_spmd(nc, in_maps, core_ids=list(range(N_CORES)))
    R = res.results
    ys = np.stack([R[b]["y"] for b in range(4)]).astype(np.float32)
    yp = np.concatenate([R[4 + g]["y"].reshape(4, 256, D) for g in range(4)], axis=0).astype(np.float32)

    def gather_new(name, H, E):
        out = np.concatenate([np.transpose(R[4 + g][name].reshape(2, 4, 256, D), (1, 0, 2, 3)) for g in range(4)],
                             axis=0)
        return np.ascontiguousarray(out.reshape(16, 2, 256, H, E)).astype(np.float32)

    return (yp, ys, gather_new("nk_diff", 8, 128), gather_new("nv_diff", 8, 128),
            gather_new("nk_na", 16, 64), gather_new("nv_na", 16, 64))
```
